# Optimizing a Trainium2 kernel written in Bass

```python
import math
import jax
import jax.numpy as jnp
from jax import lax
import numpy as np

D_MODEL = 1024
BATCH = 2
SEQ = 16384
DEPTH = 2

GRID_W = 64
CTX_LEN = 256
HEAD_DIM = 64
MIX_WIDTH = D_MODEL
GROUP_WIDTH = MIX_WIDTH // 2
N_Q_HEADS = GROUP_WIDTH // HEAD_DIM
N_KV_HEADS = N_Q_HEADS // 4
KV_WIDTH = N_KV_HEADS * HEAD_DIM
HYENA_WIDTH = GROUP_WIDTH
HYENA_ORDER = 2
HYENA_SHORT_CONV = 3
HYENA_EMB = 33
HYENA_HIDDEN = 64
HYENA_DECAY_TARGET = 1e-2
HYENA_SHORT_DECAY_PCT = 0.3
HYENA_LONG_DECAY_PCT = 1.5
Q_BLOCK = 128
WINDOW = 128
NA_ROWS = 8
NA_COLS = 16
ROPE_THETA = 10000.0
EPS = 1e-6
NEG_INF = -1e30
N_EVEN = (DEPTH + 1) // 2
N_ODD = DEPTH // 2
EVEN_SPLITS = (GROUP_WIDTH, KV_WIDTH, KV_WIDTH, (HYENA_ORDER + 1) * HYENA_WIDTH, GROUP_WIDTH, HYENA_WIDTH)
ODD_SPLITS = (GROUP_WIDTH, KV_WIDTH, KV_WIDTH, GROUP_WIDTH, GROUP_WIDTH, GROUP_WIDTH, GROUP_WIDTH, GROUP_WIDTH)
IN_WIDTH = sum(EVEN_SPLITS)

kernel_name = "hybrid_flow_backbone"


def rms_norm(x, g):
    xf = x.astype(jnp.float32)
    y = xf * lax.rsqrt(jnp.mean(xf * xf, axis=-1, keepdims=True) + EPS)
    return (y * g.astype(jnp.float32)).astype(x.dtype)


def adaln(cond, w, b):
    m = jnp.matmul(jax.nn.silu(cond), w) + b
    return jnp.split(m, 3, axis=-1)


def modulated_norm(x, g, shift, scale):
    return rms_norm(x, g) * (1 + scale) + shift


def split_cols(p, sizes):
    idx = [int(v) for v in np.cumsum(sizes)[:-1]]
    return jnp.split(p, idx, axis=-1)


def heads(t, n):
    return t.reshape(t.shape[0], t.shape[1], n, HEAD_DIM)


def axial_rope_tables(n_tokens):
    t = jnp.arange(n_tokens, dtype=jnp.int32)
    row = (t // GRID_W).astype(jnp.float32)
    col = (t % GRID_W).astype(jnp.float32)
    n_pairs = HEAD_DIM // 4
    inv = ROPE_THETA ** (-jnp.arange(n_pairs, dtype=jnp.float32) / n_pairs)
    ang = jnp.concatenate([row[:, None] * inv, col[:, None] * inv], axis=-1)
    return jnp.cos(ang), jnp.sin(ang)


def apply_rope(x, cos, sin):
    b, n, h, dh = x.shape
    xf = x.astype(jnp.float32).reshape(b, n, h, dh // 2, 2)
    x0, x1 = xf[..., 0], xf[..., 1]
    cs, sn = cos[None, :, None, :], sin[None, :, None, :]
    out = jnp.stack([x0 * cs - x1 * sn, x0 * sn + x1 * cs], axis=-1)
    return out.reshape(b, n, h, dh).astype(x.dtype)


def context_attention(q_ctx, k_ctx, v_ctx, sink_logit=None):
    b, n, h, dh = q_ctx.shape
    kv = k_ctx.shape[2]
    g = h // kv
    qg = q_ctx.reshape(b, n, kv, g, dh)
    sc = jnp.einsum('bqkgd,bckd->bkgqc', qg, k_ctx).astype(jnp.float32) * dh ** -0.5
    if sink_logit is not None:
        sink = jnp.broadcast_to(sink_logit.astype(jnp.float32).reshape(1, kv, g, 1, 1), sc.shape[:-1] + (1,))
        p = jax.nn.softmax(jnp.concatenate([sc, sink], axis=-1), axis=-1)[..., :-1]
    else:
        p = jax.nn.softmax(sc, axis=-1)
    o = jnp.einsum('bkgqc,bckd->bqkgd', p.astype(v_ctx.dtype), v_ctx)
    return o.reshape(b, n, h * dh)


def global_gqa(q, k, v, k_ctx, v_ctx):
    b, s, h, dh = q.shape
    kv = k.shape[2]
    g = h // kv
    nb = s // Q_BLOCK
    k_all = jnp.concatenate([k_ctx, k], axis=1)
    v_all = jnp.concatenate([v_ctx, v], axis=1)
    scale = dh ** -0.5
    qb = q.reshape(b, nb, Q_BLOCK, kv, g, dh).swapaxes(0, 1)

    def one_block(q_blk):
        sc = jnp.einsum('bqkgd,bnkd->bkgqn', q_blk, k_all).astype(jnp.float32) * scale
        p = jax.nn.softmax(sc, axis=-1).astype(v_all.dtype)
        return jnp.einsum('bkgqn,bnkd->bqkgd', p, v_all)

    o = lax.map(one_block, qb)
    return o.swapaxes(0, 1).reshape(b, s, h * dh)


def windowed_gqa(q, k, v, k_ctx, v_ctx, sink_logit):
    b, s, h, dh = q.shape
    kv = k.shape[2]
    g = h // kv
    nb = s // Q_BLOCK
    side = WINDOW // Q_BLOCK
    span = (2 * side + 1) * Q_BLOCK
    n_ctx = k_ctx.shape[1]
    pad = ((0, 0), (side * Q_BLOCK, side * Q_BLOCK), (0, 0), (0, 0))
    kp = jnp.pad(k, pad).reshape(b, nb + 2 * side, Q_BLOCK, kv, dh)
    vp = jnp.pad(v, pad).reshape(b, nb + 2 * side, Q_BLOCK, kv, dh)
    kb = jnp.concatenate([kp[:, j:j + nb] for j in range(2 * side + 1)], axis=2)
    vb = jnp.concatenate([vp[:, j:j + nb] for j in range(2 * side + 1)], axis=2)
    qb = q.reshape(b, nb, Q_BLOCK, kv, g, dh)
    a = jnp.arange(Q_BLOCK)[:, None]
    j = jnp.arange(span)[None, :]
    band = jnp.abs(j - side * Q_BLOCK - a) <= WINDOW
    kpos = jnp.arange(nb)[:, None] * Q_BLOCK - side * Q_BLOCK + jnp.arange(span)[None, :]
    valid = band[None] & ((kpos >= 0) & (kpos < s))[:, None, :]
    sink = sink_logit.astype(jnp.float32).reshape(kv, g, 1, 1)
    scale = dh ** -0.5

    def one_block(args):
        q_blk, k_blk, v_blk, m = args
        s_loc = jnp.einsum('bqkgd,bjkd->bkgqj', q_blk, k_blk).astype(jnp.float32) * scale
        s_loc = jnp.where(m, s_loc, NEG_INF)
        s_ctx = jnp.einsum('bqkgd,bckd->bkgqc', q_blk, k_ctx).astype(jnp.float32) * scale
        s_sink = jnp.broadcast_to(sink, s_loc.shape[:-1] + (1,))
        p = jax.nn.softmax(jnp.concatenate([s_loc, s_ctx, s_sink], axis=-1), axis=-1)
        p_loc = p[..., :span].astype(v.dtype)
        p_ctx = p[..., span:span + n_ctx].astype(v.dtype)
        return (jnp.einsum('bkgqj,bjkd->bqkgd', p_loc, v_blk)
                + jnp.einsum('bkgqc,bckd->bqkgd', p_ctx, v_ctx))

    xs = (qb.swapaxes(0, 1), kb.swapaxes(0, 1), vb.swapaxes(0, 1), valid)
    o = lax.map(one_block, xs)
    return o.swapaxes(0, 1).reshape(b, s, h * dh)


def neighbourhood_attention(q, k, v, k_ctx, v_ctx, rpb):
    b, s, h, dh = q.shape
    rows = s // GRID_W
    kr = min(NA_ROWS, rows)
    kw = min(NA_COLS, GRID_W)
    nk = kr * kw
    t = jnp.arange(s, dtype=jnp.int32)
    r = t // GRID_W
    col = t % GRID_W
    r0 = jnp.clip(r - kr // 2, 0, rows - kr)
    c0 = jnp.clip(col - kw // 2, 0, GRID_W - kw)
    key_r = r0[:, None, None] + jnp.arange(kr, dtype=jnp.int32)[None, :, None]
    key_c = c0[:, None, None] + jnp.arange(kw, dtype=jnp.int32)[None, None, :]
    key_idx = (key_r * GRID_W + key_c).reshape(s, nk)
    rel_idx = ((key_r - r[:, None, None] + NA_ROWS - 1) * (2 * NA_COLS - 1)
               + (key_c - col[:, None, None] + NA_COLS - 1)).reshape(s, nk)
    nb = s // Q_BLOCK
    rpb_flat = rpb.reshape(h, -1).astype(jnp.float32)
    scale = dh ** -0.5
    n_ctx = k_ctx.shape[1]

    def one_block(args):
        q_blk, idx, ridx = args
        kg = jnp.take(k, idx.reshape(-1), axis=1).reshape(b, Q_BLOCK, nk, h, dh)
        vg = jnp.take(v, idx.reshape(-1), axis=1).reshape(b, Q_BLOCK, nk, h, dh)
        s_loc = (jnp.einsum('bqhd,bqnhd->bhqn', q_blk, kg).astype(jnp.float32) * scale
                 + jnp.take(rpb_flat, ridx, axis=1)[None])
        s_ctx = jnp.einsum('bqhd,bchd->bhqc', q_blk, k_ctx).astype(jnp.float32) * scale
        p = jax.nn.softmax(jnp.concatenate([s_loc, s_ctx], axis=-1), axis=-1)
        p_loc = p[..., :nk].astype(v.dtype)
        p_ctx = p[..., nk:nk + n_ctx].astype(v.dtype)
        return (jnp.einsum('bhqn,bqnhd->bqhd', p_loc, vg)
                + jnp.einsum('bhqc,bchd->bqhd', p_ctx, v_ctx))

    xs = (q.reshape(b, nb, Q_BLOCK, h, dh).swapaxes(0, 1),
          key_idx.reshape(nb, Q_BLOCK, nk), rel_idx.reshape(nb, Q_BLOCK, nk))
    o = lax.map(one_block, xs)
    return o.swapaxes(0, 1).reshape(b, s, h * dh)


def hyena_positional_features(n):
    t = jnp.linspace(0.0, 1.0, n, dtype=jnp.float32)[:, None]
    bands = (HYENA_EMB - 1) // 2
    w = 2.0 * math.pi * jnp.arange(n, dtype=jnp.float32)[:, None] / n
    f = jnp.linspace(1e-4, bands - 1, bands, dtype=jnp.float32)[None, :]
    return jnp.concatenate([t, jnp.cos(f * w), -jnp.sin(f * w)], axis=-1)


def hyena_filters(n, w1, b1, w2, b2, freq, w3):
    f32 = jnp.float32
    z = hyena_positional_features(n)
    fr = freq.astype(f32)
    hdn = jnp.sin(fr * (z @ w1.astype(f32) + b1.astype(f32)))
    hdn = jnp.sin(fr * (hdn @ w2.astype(f32) + b2.astype(f32)))
    filt = hdn @ w3.astype(f32)
    t = jnp.linspace(0.0, 1.0, n, dtype=f32)[:, None]
    deltas = jnp.linspace(math.log(HYENA_DECAY_TARGET) / HYENA_LONG_DECAY_PCT,
                          math.log(HYENA_DECAY_TARGET) / HYENA_SHORT_DECAY_PCT, HYENA_WIDTH, dtype=f32)
    decay = jnp.exp(-t * jnp.abs(deltas))
    return filt.reshape(n, HYENA_ORDER, 2, HYENA_WIDTH) * decay[:, None, None, :]


def bidir_long_conv(z, h_fwd, h_bwd, d_skip):
    n = z.shape[1]
    taps = jnp.concatenate([h_fwd, jnp.zeros_like(h_fwd[:1]), h_bwd[:0:-1]], axis=0)
    taps = taps * lax.rsqrt(jnp.sum(taps * taps, axis=0, keepdims=True) + EPS)
    zf = jnp.fft.rfft(z.astype(jnp.float32), n=2 * n, axis=1)
    tf = jnp.fft.rfft(taps, n=2 * n, axis=0)
    y = jnp.fft.irfft(zf * tf[None], n=2 * n, axis=1)[:, :n]
    return (y + z.astype(jnp.float32) * d_skip.astype(jnp.float32)).astype(z.dtype)


def centred_depthwise_conv(u, w, bias):
    pad = w.shape[0] // 2
    y = lax.conv_general_dilated(u, w[:, None, :].astype(u.dtype), window_strides=(1,),
                                 padding=[(pad, pad)], dimension_numbers=('NWC', 'WIO', 'NWC'),
                                 feature_group_count=u.shape[-1])
    return y + bias


def hyena_branch(u, conv_w, conv_b, w1, b1, w2, b2, freq, w3, d_skip):
    n = u.shape[1]
    filt = hyena_filters(n, w1, b1, w2, b2, freq, w3)
    uc = centred_depthwise_conv(u, conv_w, conv_b)
    parts = jnp.split(uc, HYENA_ORDER + 1, axis=-1)
    z = parts[0]
    for o in range(HYENA_ORDER):
        z = parts[o + 1] * bidir_long_conv(z, filt[:, o, 0], filt[:, o, 1], d_skip[o])
    return z


def even_mixer(p, p_ctx, q_norm, k_norm, conv_w, conv_b, w1, b1, w2, b2, freq, w3, d_skip, cos, sin, need_ctx):
    aq, ak, av, hu, ag, hg = split_cols(p, EVEN_SPLITS)
    caq, cak, cav, chu, cag, chg = split_cols(p_ctx, EVEN_SPLITS)
    q = apply_rope(rms_norm(heads(aq, N_Q_HEADS), q_norm), cos, sin)
    k = apply_rope(rms_norm(heads(ak, N_KV_HEADS), k_norm), cos, sin)
    v = heads(av, N_KV_HEADS)
    k_c = rms_norm(heads(cak, N_KV_HEADS), k_norm)
    v_c = heads(cav, N_KV_HEADS)
    y_a = global_gqa(q, k, v, k_c, v_c)
    y_b = hyena_branch(hu, conv_w, conv_b, w1, b1, w2, b2, freq, w3, d_skip)
    y = jnp.concatenate([y_a * jax.nn.silu(ag), y_b * jax.nn.silu(hg)], axis=-1)
    if not need_ctx:
        return y, None
    q_c = rms_norm(heads(caq, N_Q_HEADS), q_norm)
    y_ca = context_attention(q_c, k_c, v_c)
    y_cb = hyena_branch(chu, conv_w, conv_b, w1, b1, w2, b2, freq, w3, d_skip)
    y_c = jnp.concatenate([y_ca * jax.nn.silu(cag), y_cb * jax.nn.silu(chg)], axis=-1)
    return y, y_c


def odd_mixer(p, p_ctx, wq_norm, wk_norm, sink, nq_norm, nk_norm, rpb, cos, sin, need_ctx):
    wq, wk, wv, nq, nk_, nv, wg, ng = split_cols(p, ODD_SPLITS)
    cwq, cwk, cwv, cnq, cnk, cnv, cwg, cng = split_cols(p_ctx, ODD_SPLITS)
    q_w = apply_rope(rms_norm(heads(wq, N_Q_HEADS), wq_norm), cos, sin)
    k_w = apply_rope(rms_norm(heads(wk, N_KV_HEADS), wk_norm), cos, sin)
    v_w = heads(wv, N_KV_HEADS)
    kc_w = rms_norm(heads(cwk, N_KV_HEADS), wk_norm)
    vc_w = heads(cwv, N_KV_HEADS)
    y_w = windowed_gqa(q_w, k_w, v_w, kc_w, vc_w, sink)
    q_n = rms_norm(heads(nq, N_Q_HEADS), nq_norm)
    k_n = rms_norm(heads(nk_, N_Q_HEADS), nk_norm)
    v_n = heads(nv, N_Q_HEADS)
    kc_n = rms_norm(heads(cnk, N_Q_HEADS), nk_norm)
    vc_n = heads(cnv, N_Q_HEADS)
    y_n = neighbourhood_attention(q_n, k_n, v_n, kc_n, vc_n, rpb)
    y = jnp.concatenate([y_w * jax.nn.silu(wg), y_n * jax.nn.silu(ng)], axis=-1)
    if not need_ctx:
        return y, None
    y_cw = context_attention(rms_norm(heads(cwq, N_Q_HEADS), wq_norm), kc_w, vc_w, sink)
    y_cn = context_attention(rms_norm(heads(cnq, N_Q_HEADS), nq_norm), kc_n, vc_n)
    y_c = jnp.concatenate([y_cw * jax.nn.silu(cwg), y_cn * jax.nn.silu(cng)], axis=-1)
    return y, y_c


def setup_inputs(seed: int = 0) -> dict:
    key = jax.random.key(seed)
    keys = iter(jax.random.split(key, 32))

    def nrm(shape, std):
        return std * jax.random.normal(next(keys), shape, jnp.float32)

    n_filter_out = HYENA_ORDER * 2 * HYENA_WIDTH
    n_hy_in = (HYENA_ORDER + 1) * HYENA_WIDTH
    return {
        'x': nrm((BATCH, SEQ, D_MODEL), 1.0),
        'c': nrm((BATCH, D_MODEL), 1.0),
        'ctx': nrm((BATCH, CTX_LEN, D_MODEL), 1.0),
        'c_ctx': nrm((D_MODEL,), 1.0),
        'norm_g': 1.0 + nrm((DEPTH, D_MODEL), 0.1),
        'w_ada': nrm((DEPTH, D_MODEL, 3 * D_MODEL), D_MODEL ** -0.5),
        'b_ada': nrm((DEPTH, 3 * D_MODEL), 0.01),
        'w_in': nrm((DEPTH, D_MODEL, IN_WIDTH), D_MODEL ** -0.5),
        'w_out': nrm((DEPTH, MIX_WIDTH, D_MODEL), MIX_WIDTH ** -0.5),
        'glob_q_norm': 1.0 + nrm((N_EVEN, HEAD_DIM), 0.1),
        'glob_k_norm': 1.0 + nrm((N_EVEN, HEAD_DIM), 0.1),
        'hy_conv_w': nrm((N_EVEN, HYENA_SHORT_CONV, n_hy_in), HYENA_SHORT_CONV ** -0.5),
        'hy_conv_b': nrm((N_EVEN, n_hy_in), 0.01),
        'hy_w1': nrm((N_EVEN, HYENA_EMB, HYENA_HIDDEN), HYENA_EMB ** -0.5),
        'hy_b1': nrm((N_EVEN, HYENA_HIDDEN), 0.01),
        'hy_w2': nrm((N_EVEN, HYENA_HIDDEN, HYENA_HIDDEN), HYENA_HIDDEN ** -0.5),
        'hy_b2': nrm((N_EVEN, HYENA_HIDDEN), 0.01),
        'hy_freq': 1.0 + nrm((N_EVEN, HYENA_HIDDEN), 0.1),
        'hy_w3': nrm((N_EVEN, HYENA_HIDDEN, n_filter_out), HYENA_HIDDEN ** -0.5),
        'hy_skip': nrm((N_EVEN, HYENA_ORDER, HYENA_WIDTH), 0.5),
        'win_q_norm': 1.0 + nrm((N_ODD, HEAD_DIM), 0.1),
        'win_k_norm': 1.0 + nrm((N_ODD, HEAD_DIM), 0.1),
        'win_sink': nrm((N_ODD, N_Q_HEADS), 0.5),
        'nat_q_norm': 1.0 + nrm((N_ODD, HEAD_DIM), 0.1),
        'nat_k_norm': 1.0 + nrm((N_ODD, HEAD_DIM), 0.1),
        'nat_rpb': nrm((N_ODD, N_Q_HEADS, 2 * NA_ROWS - 1, 2 * NA_COLS - 1), 0.1),
    }


def reference(x, c, ctx, c_ctx, norm_g, w_ada, b_ada, w_in, w_out,
              glob_q_norm, glob_k_norm, hy_conv_w, hy_conv_b, hy_w1, hy_b1, hy_w2, hy_b2,
              hy_freq, hy_w3, hy_skip, win_q_norm, win_k_norm, win_sink,
              nat_q_norm, nat_k_norm, nat_rpb):
    cos, sin = axial_rope_tables(x.shape[1])
    x_ctx = ctx
    for layer in range(DEPTH):
        need_ctx = layer < DEPTH - 1
        shift, scale, gate = adaln(c, w_ada[layer], b_ada[layer])
        c_shift, c_scale, c_gate = adaln(c_ctx, w_ada[layer], b_ada[layer])
        h = modulated_norm(x, norm_g[layer], shift[:, None], scale[:, None])
        h_ctx = modulated_norm(x_ctx, norm_g[layer], c_shift, c_scale)
        p = jnp.matmul(h, w_in[layer])
        p_ctx = jnp.matmul(h_ctx, w_in[layer])
        i = layer // 2
        if layer % 2 == 0:
            y, y_ctx = even_mixer(p, p_ctx, glob_q_norm[i], glob_k_norm[i], hy_conv_w[i], hy_conv_b[i],
                                  hy_w1[i], hy_b1[i], hy_w2[i], hy_b2[i], hy_freq[i], hy_w3[i], hy_skip[i],
                                  cos, sin, need_ctx)
        else:
            y, y_ctx = odd_mixer(p, p_ctx, win_q_norm[i], win_k_norm[i], win_sink[i],
                                 nat_q_norm[i], nat_k_norm[i], nat_rpb[i], cos, sin, need_ctx)
        x = x + gate[:, None] * jnp.matmul(y, w_out[layer])
        if need_ctx:
            x_ctx = x_ctx + c_gate * jnp.matmul(y_ctx, w_out[layer])
    return x
```

```python
import contextlib
import numpy as np
import ml_dtypes
import concourse.bass as bass
import concourse.mybir as mybir
from concourse.bass_utils import run_bass_kernel_spmd

F32 = mybir.dt.float32
BF16 = mybir.dt.bfloat16
AF = mybir.ActivationFunctionType
ALU = mybir.AluOpType
AX = mybir.AxisListType

COMPUTE = ("pe", "act", "dve", "pool")
NRING = 24


class Op:
    __slots__ = ("eng", "fn", "reads", "writes", "dma", "deps", "sig", "cnt", "sem", "val")

    def __init__(self, eng, fn, reads, writes, dma):
        self.eng = eng
        self.fn = fn
        self.reads = tuple(reads)
        self.writes = tuple(writes)
        self.dma = dma
        self.deps = ()
        self.sig = False
        self.cnt = 0
        self.sem = None
        self.val = 0


class Prog:
    def __init__(self, nc, same_engine_sync=True):
        self.nc = nc
        self.ops = []
        self.same_engine_sync = same_engine_sync

    def op(self, eng, fn, reads=(), writes=()):
        self.ops.append(Op(eng, fn, reads, writes, False))

    def dma(self, q, out, in_, reads=(), writes=(), **kw):
        self.ops.append(Op(q, lambda e: e.dma_start(out=out, in_=in_, **kw), reads, writes, True))

    def analyze(self):
        last_w = {}
        readers = {}
        bank_last = {}
        ndma = 0
        dma_ops = []
        for i, o in enumerate(self.ops):
            deps = set()
            for t in o.reads:
                if t in last_w:
                    deps.add(last_w[t])
            for t in o.writes:
                if t in last_w:
                    deps.add(last_w[t])
                for r in readers.get(t, ()):
                    deps.add(r)
            for t in set(o.reads) | set(o.writes):
                if isinstance(t, str) and t.startswith("PS:"):
                    bl = bank_last.setdefault(t, {})
                    for e2, j2 in bl.items():
                        if e2 != o.eng:
                            deps.add(j2)
                    bl[o.eng] = i
            if o.dma:
                if ndma >= NRING:
                    deps.add(dma_ops[ndma - NRING])
                dma_ops.append(i)
                o.sem = ndma % NRING
                o.val = 16 * (ndma // NRING + 1)
                ndma += 1
            deps.discard(i)
            keep = []
            for j in deps:
                oj = self.ops[j]
                if (not oj.dma) and oj.eng == o.eng and (not o.dma or True):
                    if oj.eng == "pe" or not self.same_engine_sync:
                        continue
                keep.append(j)
                if not oj.dma:
                    oj.sig = True
            o.deps = tuple(sorted(keep))
            for t in o.reads:
                readers.setdefault(t, []).append(i)
            for t in o.writes:
                last_w[t] = i
                readers[t] = []
        cnt = {e: 0 for e in COMPUTE + ("sp",)}
        for o in self.ops:
            if not o.dma and o.sig:
                cnt[o.eng] += 1
                o.cnt = cnt[o.eng]
        self.ndma = ndma
        self.final_cnt = cnt

    def emit(self, stack):
        nc = self.nc
        self.analyze()
        esem = {e: stack.enter_context(nc.semaphore("s_" + e)) for e in COMPUTE + ("sp",)}
        ring = [stack.enter_context(nc.semaphore("d%d" % i)) for i in range(NRING)]
        block = stack.enter_context(nc.Block())
        ops = self.ops
        ring_final = {}
        for o in ops:
            if o.dma:
                ring_final[o.sem] = max(ring_final.get(o.sem, 0), o.val)

        def stream(engname, eng):
            waited = {}
            for o in ops:
                if o.eng != engname:
                    continue
                for j in o.deps:
                    oj = ops[j]
                    if oj.dma:
                        s, v, key = ring[oj.sem], oj.val, ("d", oj.sem)
                    else:
                        s, v, key = esem[oj.eng], oj.cnt, ("e", oj.eng)
                    if waited.get(key, 0) >= v:
                        continue
                    waited[key] = v
                    eng.wait_ge(s, v)
                ins = o.fn(eng)
                if o.dma:
                    ins.then_inc(ring[o.sem], 16)
                elif o.sig:
                    ins.then_inc(esem[o.eng], 1)
            if engname == "sp":
                for s, v in sorted(ring_final.items()):
                    if waited.get(("d", s), 0) < v:
                        eng.wait_ge(ring[s], v)

        @block.sync
        def _(e):
            stream("sp", e)

        @block.tensor
        def _(e):
            stream("pe", e)

        @block.scalar
        def _(e):
            stream("act", e)

        @block.vector
        def _(e):
            stream("dve", e)

        @block.gpsimd
        def _(e):
            stream("pool", e)


D = 1024
SEQ = 16384
NB = 2
CTX = 256
INW = 3328
NCH = INW // 128
TOK = 4096
EPS = 1e-6
BF = ml_dtypes.bfloat16

CH_TYPES = [
    ['rope'] * 5 + ['plain'] * 13 + ['silu'] * 8,
    ['rope'] * 5 + ['plain'] + ['norm'] * 8 + ['plain'] * 4 + ['silu'] * 8,
]


class K:
    def __init__(self):
        self.nc = bass.Bass("TRN2", target_bir_lowering=False)
        self.st = contextlib.ExitStack()
        self.P = Prog(self.nc)

    def inp(self, name, shape, dt):
        return self.nc.dram_tensor(name, list(shape), dt, kind="ExternalInput").ap()

    def outp(self, name, shape, dt):
        return self.nc.dram_tensor(name, list(shape), dt, kind="ExternalOutput").ap()

    def sb(self, name, shape, dt):
        return self.st.enter_context(self.nc.sbuf_tensor(name, list(shape), dt))

    def ps(self, name, shape, dt):
        return self.st.enter_context(self.nc.psum_tensor(name, list(shape), dt))

    def finish(self):
        self.P.emit(self.st)
        self.st.close()
        return self.nc


def build_P(layer):
    k = K()
    nc, P = k.nc, k.P
    types = CH_TYPES[layer]
    x = k.inp("x", [TOK, D], F32)
    cx = k.inp("cx", [2 * CTX, D], F32)
    cT = k.inp("cT", [128, 8, 2], F32)
    w_ada = k.inp("w_ada", [D, 3 * D], F32)
    b_ada = k.inp("b_ada", [128, 24, 2], F32)
    ng = k.inp("ng", [128, 8], F32)
    w_in = k.inp("w_in", [D, INW], F32)
    gains = k.inp("gains", [128, NCH], F32)
    cosT = k.inp("cosT", [128, TOK], F32)
    sinT = k.inp("sinT", [128, TOK], F32)
    consts = k.inp("consts", [128, 3, 128], BF16)
    pT = k.outp("pT", [INW, TOK + 2 * CTX], BF16)
    modo = k.outp("modo", [128, 24, 2], F32)

    cst = k.sb("cst", [128, 3, 128], BF16)
    sc = k.sb("sc", [128, 8, 2], F32)
    wa = [k.sb("wa%d" % i, [128, 8, 128], F32) for i in range(2)]
    bad = k.sb("bad", [128, 24, 2], F32)
    mod = k.sb("mod", [128, 24, 2], F32)
    ngs = k.sb("ngs", [128, 8], F32)
    Asc = k.sb("Asc", [128, 8, 2], F32)
    gsb = k.sb("gsb", [128, NCH], F32)
    w_sb = k.sb("w_sb", [128, 8, INW], BF16)
    wst = [k.sb("wst%d" % i, [128, INW // 2], F32) for i in range(2)]
    cs = k.sb("cs", [128, TOK], F32)
    sn = k.sb("sn", [128, TOK], F32)
    xt = [k.sb("xt%d" % i, [128, D], F32) for i in range(3)]
    xn = [k.sb("xn%d" % i, [128, D], BF16) for i in range(2)]
    junk = k.sb("junk", [128, D], BF16)
    ssq = [k.sb("ssq%d" % i, [128, 1], F32) for i in range(3)]
    rstd = [k.sb("rstd%d" % i, [128, 1], F32) for i in range(3)]
    hT = [k.sb("hT%d" % i, [128, 8, 512], BF16) for i in range(2)]
    sq = [k.sb("sq%d" % i, [128, 512], BF16) for i in range(2)]
    rs = [k.sb("rs%d" % i, [128, 512], F32) for i in range(2)]
    qnb = [k.sb("qnb%d" % i, [128, 512], BF16) for i in range(2)]
    t1 = [k.sb("t1%d" % i, [128, 512], F32) for i in range(2)]
    t2 = [k.sb("t2%d" % i, [128, 512], F32) for i in range(2)]
    oT = [k.sb("oT%d" % i, [128, 512], BF16) for i in range(3)]
    mod_ps_full = k.ps("mod_ps", [128, 512], F32)
    mod_ps = mod_ps_full[:, 0:48].rearrange("p (a b) -> p a b", b=2)
    tp = [k.ps("tp%d" % i, [128, 8, 128], BF16) for i in range(2)]
    pp = [k.ps("pp%d" % i, [128, 512], F32) for i in range(2)]
    ss = [k.ps("ss%d" % i, [128, 512], F32) for i in range(2)]

    P.dma("sp", cst[:], consts, writes=["cst"])
    P.dma("sp", sc[:], cT, writes=["sc"])
    P.dma("sp", bad[:], b_ada, writes=["bad"])
    P.dma("sp", ngs[:], ng, writes=["ngs"])
    P.dma("sp", gsb[:], gains, writes=["gsb"])
    P.dma("pool", cs[:], cosT, writes=["cs"])
    P.dma("pool", sn[:], sinT, writes=["sn"])
    ident = cst[:, 0, :]
    pswap = cst[:, 1, :]
    onesb = cst[:, 2, :]

    P.op("act", lambda e: e.activation(out=sc[:], in_=sc[:], func=AF.Silu), reads=["sc"], writes=["sc"])
    w_ada_v = w_ada.rearrange("(kc p) n -> p kc n", p=128)
    for oc in range(24):
        wt = wa[oc % 2]
        tg = "wa%d" % (oc % 2)
        P.dma("sp", wt[:], w_ada_v[:, :, oc * 128:(oc + 1) * 128], writes=[tg])
        for kc in range(8):
            P.op("pe", (lambda e, wt=wt, kc=kc, oc=oc: e.matmul(mod_ps[:, oc, :], lhsT=wt[:, kc, :], rhs=sc[:, kc, :],
                                                             start=(kc == 0), stop=(kc == 7))),
                 reads=[tg, "sc"], writes=["PS:mod_ps"])
    P.op("dve", lambda e: e.tensor_tensor(out=mod[:], in0=mod_ps[:], in1=bad[:], op=ALU.add),
         reads=["PS:mod_ps", "bad"], writes=["mod"])
    P.dma("sp", modo, mod[:], reads=["mod"])
    for j in range(2):
        P.op("dve", (lambda e, j=j: e.scalar_tensor_tensor(out=Asc[:, :, j], in0=mod[:, 8:16, j], scalar=1.0, in1=ngs[:],
                                                        op0=ALU.add, op1=ALU.mult)),
             reads=["mod", "ngs"], writes=["Asc"])

    import os
    STAGE = int(os.environ.get("PSTAGE", "9"))
    if STAGE < 2:
        return k.finish()
    w_in_v = w_in.rearrange("(kc p) n -> p kc n", p=128)
    H = INW // 2
    for kc in range(8):
        for hh in range(2):
            i = (kc * 2 + hh) % 2
            P.dma("act" if hh else "sp", wst[i][:], w_in_v[:, kc, hh * H:(hh + 1) * H], writes=["wst%d" % i])
            eng = "pool" if hh else "dve"
            P.op(eng, (lambda e, i=i, kc=kc, hh=hh: e.tensor_copy(out=w_sb[:, kc, hh * H:(hh + 1) * H], in_=wst[i][:])),
                 reads=["wst%d" % i], writes=["w_sb%d_%d" % (kc, hh)])
    WTAGS = ["w_sb%d_%d" % (kc, hh) for kc in range(8) for hh in range(2)]

    nblk = 0
    nchunk = 0
    if STAGE < 3:
        return k.finish()
    for t in range(9 if STAGE >= 9 else 1):
        is_ctx = (t == 8)
        j = 1 if is_ctx else 0
        hb = hT[t % 2]
        htag = "hT%d" % (t % 2)
        for s in range(4):
            xi = nblk % 3
            xni = nblk % 2
            tpi = nblk % 2
            src = cx[s * 128:(s + 1) * 128, :] if is_ctx else x[t * 512 + s * 128: t * 512 + (s + 1) * 128, :]
            P.dma("sp", xt[xi][:], src, writes=["xt%d" % xi])
            P.op("act", (lambda e, xi=xi: e.activation(out=junk[:], in_=xt[xi][:], func=AF.Square, accum_out=ssq[xi][:])),
                 reads=["xt%d" % xi], writes=["junk", "ssq%d" % xi])
            PSUB = int(os.environ.get("PSUB", "9"))
            if PSUB < 2:
                continue
            P.op("act", (lambda e, xi=xi: e.activation(out=rstd[xi][:], in_=ssq[xi][:], func=AF.Sqrt, scale=1.0 / D, bias=EPS)),
                 reads=["ssq%d" % xi], writes=["rstd%d" % xi])
            P.op("dve", (lambda e, xi=xi: e.reciprocal(out=rstd[xi][:], in_=rstd[xi][:])),
                 reads=["rstd%d" % xi], writes=["rstd%d" % xi])
            if PSUB < 3:
                continue
            P.op("dve", (lambda e, xi=xi, xni=xni: e.tensor_scalar(out=xn[xni][:], in0=xt[xi][:], scalar1=rstd[xi][:], scalar2=None,
                                                                 op0=ALU.mult)),
                 reads=["xt%d" % xi, "rstd%d" % xi], writes=["xn%d" % xni])
            if PSUB < 4:
                continue
            for kc in range(8):
                P.op("pe", (lambda e, kc=kc, xni=xni, tpi=tpi: e.transpose(tp[tpi][:, kc, :], xn[xni][:, kc * 128:(kc + 1) * 128], ident)),
                     reads=["xn%d" % xni, "cst"], writes=["PS:tp%d" % tpi])
            if PSUB < 5:
                continue
            for kc in range(8):
                eng = "dve" if tpi == 0 else "act"
                if eng == "dve":
                    fn = (lambda e, kc=kc, tpi=tpi, s=s, j=j, hb=hb: e.tensor_scalar(
                        out=hb[:, kc, s * 128:(s + 1) * 128], in0=tp[tpi][:, kc, :], scalar1=Asc[:, kc, j:j + 1],
                        scalar2=mod[:, kc, j:j + 1], op0=ALU.mult, op1=ALU.add))
                else:
                    fn = (lambda e, kc=kc, tpi=tpi, s=s, j=j, hb=hb: e.activation(
                        out=hb[:, kc, s * 128:(s + 1) * 128], in_=tp[tpi][:, kc, :], func=AF.Identity,
                        scale=Asc[:, kc, j:j + 1], bias=mod[:, kc, j:j + 1]))
                P.op(eng, fn, reads=["PS:tp%d" % tpi, "Asc", "mod"], writes=[htag + "_%d" % kc])
            nblk += 1
        HT = [htag + "_%d" % kc for kc in range(8)]
        tcol = (TOK + 0) if is_ctx else t * 512
        for cc in range(NCH):
            ty = types[cc]
            if STAGE == 3:
                break
            if STAGE == 4 and ty != 'plain':
                continue
            if STAGE == 5 and ty == 'rope':
                ty = 'norm'
            if is_ctx and ty == 'rope':
                ty = 'norm'
            pi = nchunk % 2
            oi = nchunk % 3
            ppt = pp[pi]
            ptag = "PS:pp%d" % pi
            for kc in range(8):
                P.op("pe", (lambda e, kc=kc, cc=cc, ppt=ppt, hb=hb: e.matmul(ppt[:], lhsT=w_sb[:, kc, cc * 128:(cc + 1) * 128],
                                                                           rhs=hb[:, kc, :], start=(kc == 0), stop=(kc == 7))),
                     reads=[HT[kc], "w_sb%d_%d" % (kc, cc // 13)], writes=[ptag])
            ob = oT[oi]
            otag = "oT%d" % oi
            if ty == 'plain':
                if cc % 2 == 0:
                    P.op("act", (lambda e, ppt=ppt, ob=ob: e.activation(out=ob[:], in_=ppt[:], func=AF.Identity)),
                         reads=[ptag], writes=[otag])
                else:
                    P.op("dve", (lambda e, ppt=ppt, ob=ob: e.tensor_copy(out=ob[:], in_=ppt[:])), reads=[ptag], writes=[otag])
            elif ty == 'silu':
                P.op("act", (lambda e, ppt=ppt, ob=ob: e.activation(out=ob[:], in_=ppt[:], func=AF.Silu)),
                     reads=[ptag], writes=[otag])
            else:
                P.op("act", (lambda e, ppt=ppt, pi=pi: e.activation(out=sq[pi][:], in_=ppt[:], func=AF.Square)),
                     reads=[ptag], writes=["sq%d" % pi])
                P.op("pe", (lambda e, pi=pi: e.matmul(ss[pi][:], lhsT=onesb, rhs=sq[pi][:], start=True, stop=True)),
                     reads=["sq%d" % pi, "cst"], writes=["PS:ss%d" % pi])
                P.op("act", (lambda e, pi=pi: e.activation(out=rs[pi][:], in_=ss[pi][:], func=AF.Sqrt, scale=1.0 / 64, bias=EPS)),
                     reads=["PS:ss%d" % pi], writes=["rs%d" % pi])
                P.op("dve", (lambda e, pi=pi: e.reciprocal(out=rs[pi][:], in_=rs[pi][:])), reads=["rs%d" % pi], writes=["rs%d" % pi])
                dst = ob if ty == 'norm' else qnb[pi]
                dtag = otag if ty == 'norm' else "qnb%d" % pi
                P.op("dve", (lambda e, ppt=ppt, pi=pi, cc=cc, dst=dst: e.scalar_tensor_tensor(
                    out=dst[:], in0=ppt[:], scalar=gsb[:, cc:cc + 1], in1=rs[pi][:], op0=ALU.mult, op1=ALU.mult)),
                    reads=[ptag, "gsb", "rs%d" % pi], writes=[dtag])
                if ty == 'rope':
                    P.op("pe", (lambda e, pi=pi: e.matmul(ss[pi][:], lhsT=pswap, rhs=qnb[pi][:], start=True, stop=True)),
                         reads=["qnb%d" % pi, "cst"], writes=["PS:ss%d" % pi])
                    P.op("pool", (lambda e, pi=pi, t=t: e.tensor_tensor(out=t1[pi][:], in0=qnb[pi][:], in1=cs[:, t * 512:(t + 1) * 512],
                                                                      op=ALU.mult)),
                         reads=["qnb%d" % pi, "cs"], writes=["t1%d" % pi])
                    P.op("dve", (lambda e, pi=pi, t=t: e.tensor_tensor(out=t2[pi][:], in0=ss[pi][:], in1=sn[:, t * 512:(t + 1) * 512],
                                                                     op=ALU.mult)),
                         reads=["PS:ss%d" % pi, "sn"], writes=["t2%d" % pi])
                    P.op("dve", (lambda e, pi=pi, ob=ob: e.tensor_tensor(out=ob[:], in0=t1[pi][:], in1=t2[pi][:], op=ALU.add)),
                         reads=["t1%d" % pi, "t2%d" % pi], writes=[otag])
            P.dma("sp" if cc % 2 else "pool", pT[cc * 128:(cc + 1) * 128, tcol:tcol + 512], ob[:], reads=[otag])
            nchunk += 1
    return k.finish()


def rope_tables():
    t = np.arange(SEQ)
    row = (t // 64).astype(np.float32)
    col = (t % 64).astype(np.float32)
    inv = (10000.0 ** (-np.arange(16, dtype=np.float32) / 16)).astype(np.float32)
    ang = np.concatenate([row[:, None] * inv, col[:, None] * inv], axis=-1).astype(np.float32)
    cos = np.cos(ang).astype(np.float32)
    sin = np.sin(ang).astype(np.float32)
    cosT = np.repeat(cos, 2, axis=1).T
    sgn = np.tile(np.array([-1.0, 1.0], np.float32), 32)[:, None]
    sinT = np.repeat(sin, 2, axis=1).T * sgn
    cosT = np.ascontiguousarray(np.concatenate([cosT, cosT], 0))
    sinT = np.ascontiguousarray(np.concatenate([sinT, sinT], 0))
    return cosT.astype(np.float32), sinT.astype(np.float32)


def make_consts():
    c = np.zeros((128, 3, 128), np.float32)
    c[:, 0, :] = np.eye(128)
    for i in range(128):
        c[i, 1, i ^ 1] = 1.0
    c[:64, 2, :64] = 1.0
    c[64:, 2, 64:] = 1.0
    return c.astype(BF)


def col_vec(v, n):
    return np.ascontiguousarray(np.asarray(v, np.float32).reshape(n, 128).T)


_CACHE = {}


def get_prog(name, builder, *args):
    key = (name,) + tuple(args)
    if key not in _CACHE:
        _CACHE[key] = builder(*args)
    return _CACHE[key]


def run_P(layer, x, xctx, inp):
    nc = get_prog("P", build_P, layer)
    cosT, sinT = rope_tables()
    consts = make_consts()
    if layer == 0:
        glist = [inp['glob_q_norm'][0]] * 4 + [inp['glob_k_norm'][0]] + [np.ones(64, np.float32)] * 21
    else:
        glist = ([inp['win_q_norm'][0]] * 4 + [inp['win_k_norm'][0]] + [np.ones(64, np.float32)] + [inp['nat_q_norm'][0]] * 4
                 + [inp['nat_k_norm'][0]] * 4 + [np.ones(64, np.float32)] * 12)
    gains = np.ascontiguousarray(np.stack([np.tile(np.asarray(g, np.float32), 2) for g in glist], axis=1))
    b_ada = np.ascontiguousarray(np.repeat(col_vec(inp['b_ada'][layer], 24)[:, :, None], 2, axis=2))
    ng = col_vec(inp['norm_g'][layer], 8)
    cxs = np.ascontiguousarray(xctx.reshape(2 * CTX, D))
    w_ada = np.ascontiguousarray(inp['w_ada'][layer])
    w_in = np.ascontiguousarray(inp['w_in'][layer])
    maps = []
    for core in range(8):
        b, j = core // 4, core % 4
        cT = np.ascontiguousarray(np.stack([col_vec(inp['c'][b], 8), col_vec(inp['c_ctx'], 8)], axis=2))
        maps.append({
            "x": np.ascontiguousarray(x[b, j * TOK:(j + 1) * TOK]), "cx": cxs, "cT": cT, "w_ada": w_ada, "b_ada": b_ada, "ng": ng,
            "w_in": w_in, "gains": gains, "cosT": np.ascontiguousarray(cosT[:, j * TOK:(j + 1) * TOK]),
            "sinT": np.ascontiguousarray(sinT[:, j * TOK:(j + 1) * TOK]), "consts": consts,
        })
    import os
    if int(os.environ.get("PSTAGE", "9")) < 9:
        res = run_bass_kernel_spmd(nc, maps[:1], core_ids=[0]).results
        res = res * 8
    else:
        res = run_bass_kernel_spmd(nc, maps, core_ids=list(range(8))).results
    pT = np.empty((INW, 2, SEQ), BF)
    for core in range(8):
        b, j = core // 4, core % 4
        pT[:, b, j * TOK:(j + 1) * TOK] = res[core]["pT"][:, :TOK]
    pTc = np.ascontiguousarray(res[0]["pT"][:, TOK:].reshape(INW, 2, CTX))
    mod = np.stack([res[0]["modo"][:, :, 0], res[4]["modo"][:, :, 0]], axis=0)
    cmod = res[0]["modo"][:, :, 1]
    return pT, pTc, mod, cmod


def build_G(name, TQ, TW, NKB, NQS, NKS, NVS, npass, heads, blocks_fn, NE, use_sink):
    k = K()
    nc, P = k.nc, k.P
    NH = sum(len(h) for h in heads)
    qT = k.inp("qT", [npass, 128, NQS, TQ], BF16)
    kT = k.inp("kT", [npass, 128, NKS, NKB * 128], BF16)
    va = k.inp("va", [npass, 128, NKB, NVS, 128], BF16)
    shm = k.inp("shm", [128, 64], F32)
    if NE:
        Bt = k.inp("Bt", [128, NE, TW], F32)
    if use_sink:
        snk = k.inp("snk", [128, 8], F32)
    yT = k.outp("yT", [NH * 64, TQ], BF16)

    q_sb = k.sb("q_sb", [128, NQS, TQ], BF16)
    k_sb = k.sb("k_sb", [128, NKS, NKB * 128], BF16)
    v_sb = k.sb("v_sb", [128, NKB, NVS, 128], BF16)
    shs = k.sb("shs", [128, 64], F32)
    NPB = 4
    pT = [k.sb("pT%d" % i, [128, 512], BF16) for i in range(NPB)]
    R = [k.sb("R%d" % i, [128, TW], F32) for i in range(2)]
    Rs = [k.sb("Rs%d" % i, [64, TW], F32) for i in range(2)]
    yo = [k.sb("yo%d" % i, [64, TW], BF16) for i in range(2)]
    if NE:
        E_sb = k.sb("E_sb", [128, NE, TW], BF16)
        Bst = [k.sb("Bst%d" % i, [128, 8, TW], F32) for i in range(2)]
    if use_sink:
        es = k.sb("es", [128, 8], F32)
    NSB = 3
    S_ps = [k.ps("S%d" % i, [128, 512], F32) for i in range(NSB)]
    O_ps = [k.ps("O%d" % i, [128, 512], F32) for i in range(2)]
    Rp = [k.ps("Rp%d" % i, [128, 512], F32) for i in range(2)]

    P.dma("sp", shs[:], shm, writes=["shs"])
    for ri in range(2):
        P.op("dve", (lambda e, ri=ri: e.memset(R[ri][:], 0.0)), writes=["R%d" % ri])
    if use_sink:
        P.dma("sp", es[:], snk, writes=["es"])
        P.op("act", lambda e: e.activation(out=es[:], in_=es[:], func=AF.Exp), reads=["es"], writes=["es"])
    if NE:
        for i0 in range(0, NE, 8):
            n = min(8, NE - i0)
            bi = (i0 // 8) % 2
            P.dma("sp", Bst[bi][:, 0:n, :], Bt[:, i0:i0 + n, :], writes=["Bst%d" % bi])
            P.op("act", (lambda e, bi=bi, i0=i0, n=n: e.activation(out=E_sb[:, i0:i0 + n, :], in_=Bst[bi][:, 0:n, :], func=AF.Exp)),
                 reads=["Bst%d" % bi], writes=["E_sb"])

    ntile = TQ // TW
    GB = 512 // TW
    LA = 2
    groups = []
    units = []
    for ps_ in range(npass):
        for hd in heads[ps_]:
            for t in range(ntile):
                blks = blocks_fn(t)
                u = len(units)
                units.append((ps_, hd, t))
                ng = (len(blks) + GB - 1) // GB
                for gg in range(ng):
                    grp = blks[gg * GB:(gg + 1) * GB]
                    groups.append(dict(u=u, ps=ps_, hd=hd, t=t, grp=grp, first=(gg == 0), last=(gg == ng - 1),
                                       newpass=(gg == 0 and t == 0 and hd is heads[ps_][0])))

    def load_pass(ps_):
        for s_ in range(NQS):
            P.dma("sp", q_sb[:, s_, :], qT[ps_, :, s_, :], writes=["q_sb"])
        for s_ in range(NKS):
            P.dma("pool", k_sb[:, s_, :], kT[ps_, :, s_, :], writes=["k_sb"])
        step = max(1, 16 // NVS)
        for b0 in range(0, NKB, step):
            b1 = min(NKB, b0 + step)
            P.dma("sp" if (b0 // step) % 2 else "pool", v_sb[:, b0:b1], va[ps_, :, b0:b1], writes=["v_sb"])

    def emit_qk(n, G):
        (qs, base, ks, vs, oh, si, bh) = G["hd"]
        St = S_ps[n % NSB]
        stag = "PS:S%d" % (n % NSB)
        t = G["t"]
        for gi, (kb, ev) in enumerate(G["grp"]):
            P.op("pe", (lambda e, St=St, gi=gi, kb=kb, ks=ks, qs=qs, t=t: e.matmul(
                St[:, gi * TW:(gi + 1) * TW], lhsT=k_sb[:, ks, kb * 128:(kb + 1) * 128],
                rhs=q_sb[:, qs, t * TW:(t + 1) * TW], start=True, stop=True)),
                reads=["k_sb", "q_sb"], writes=[stag])

    def emit_rest(n, G):
        (qs, base, ks, vs, oh, si, bh) = G["hd"]
        St = S_ps[n % NSB]
        stag = "PS:S%d" % (n % NSB)
        pb_i = n % NPB
        ptag = "pT%d" % pb_i
        ob = G["u"] % 2
        Ot = O_ps[ob]
        otag = "PS:O%d" % ob
        grp = G["grp"]
        w = len(grp) * TW
        P.op("act", (lambda e, St=St, pb_i=pb_i, w=w: e.activation(out=pT[pb_i][:, 0:w], in_=St[:, 0:w], func=AF.Exp, scale=0.125)),
             reads=[stag], writes=[ptag])
        for gi, (kb, ev) in enumerate(grp):
            if ev is not None:
                ei = ev(bh)
                P.op("dve", (lambda e, pb_i=pb_i, gi=gi, ei=ei: e.tensor_tensor(
                    out=pT[pb_i][:, gi * TW:(gi + 1) * TW], in0=pT[pb_i][:, gi * TW:(gi + 1) * TW], in1=E_sb[:, ei, :], op=ALU.mult)),
                    reads=[ptag, "E_sb"], writes=[ptag])
        for gi, (kb, ev) in enumerate(grp):
            first = G["first"] and gi == 0
            last = G["last"] and gi == len(grp) - 1
            P.op("pe", (lambda e, Ot=Ot, kb=kb, vs=vs, pb_i=pb_i, gi=gi, first=first, last=last: e.matmul(
                Ot[:, 0:TW], lhsT=v_sb[:, kb, vs, :], rhs=pT[pb_i][:, gi * TW:(gi + 1) * TW], start=first, stop=last)),
                reads=["v_sb", ptag], writes=[otag])

    def emit_norm_a(u):
        (ps_, hd, t) = units[u]
        (qs, base, ks, vs, oh, si, bh) = hd
        Ot = O_ps[u % 2]
        otag = "PS:O%d" % (u % 2)
        ri = u % 2
        rt = "R%d" % ri
        if si is not None:
            P.op("dve", (lambda e, Ot=Ot, si=si, ri=ri: e.tensor_scalar(out=R[ri][64:128, :], in0=Ot[64:128, 0:TW], scalar1=es[64:128, si:si + 1],
                                                                   scalar2=None, op0=ALU.add)),
                 reads=[otag, "es"], writes=[rt])
            P.op("dve", (lambda e, ri=ri: e.reciprocal(out=R[ri][64:128, :], in_=R[ri][64:128, :])), reads=[rt], writes=[rt])
        else:
            P.op("dve", (lambda e, Ot=Ot, ri=ri: e.reciprocal(out=R[ri][64:128, :], in_=Ot[64:128, 0:TW])), reads=[otag], writes=[rt])

    def emit_norm_b(u):
        (ps_, hd, t) = units[u]
        (qs, base, ks, vs, oh, si, bh) = hd
        Ot = O_ps[u % 2]
        otag = "PS:O%d" % (u % 2)
        ri = u % 2
        P.op("pe", (lambda e, ri=ri: e.matmul(Rp[ri][0:64, 0:TW], lhsT=shs[:], rhs=R[ri][:], start=True, stop=True)),
             reads=["shs", "R%d" % ri], writes=["PS:Rp%d" % ri])
        P.op("act", (lambda e, ri=ri: e.activation(out=Rs[ri][:], in_=Rp[ri][0:64, 0:TW], func=AF.Identity)), reads=["PS:Rp%d" % ri], writes=["Rs%d" % ri])
        P.op("dve", (lambda e, ri=ri, Ot=Ot: e.tensor_tensor(out=yo[ri][:], in0=Ot[0:64, 0:TW], in1=Rs[ri][:], op=ALU.mult)),
             reads=[otag, "Rs%d" % ri], writes=["yo%d" % ri])
        P.dma("sp", yT[oh * 64:(oh + 1) * 64, t * TW:(t + 1) * TW], yo[ri][:], reads=["yo%d" % ri])

    gbase = 0
    for ps_ in range(npass):
        gl = [G for G in groups if G["ps"] == ps_]
        N = len(gl)
        load_pass(ps_)
        pend_b = {}
        for idx in range(N + LA + 4):
            if idx < N:
                emit_qk(gbase + idx, gl[idx])
            for u in pend_b.pop(idx, []):
                emit_norm_b(u)
            j = idx - LA
            if 0 <= j < N:
                G = gl[j]
                emit_rest(gbase + j, G)
                if G["last"]:
                    emit_norm_a(G["u"])
                    pend_b.setdefault(idx + 2, []).append(G["u"])
        gbase += N
    return k.finish()


def shift_mat():
    m = np.zeros((128, 64), np.float32)
    for i in range(64):
        m[64 + i, i] = 1.0
    return m


def make_vaug(vT_rows, nkb):
    nv = vT_rows.shape[0] // 64
    v = vT_rows.view(np.uint16).reshape(nv, 64, nkb, 128)
    out = np.empty((128, nkb, nv, 128), np.uint16)
    out[:, :, :, :64] = v.transpose(3, 2, 0, 1)
    out[:, :, :, 64:] = np.array([1.0], BF).view(np.uint16)[0]
    return out.view(BF)


def gqa_q_layout(qrows):
    T = qrows.shape[1]
    q = qrows.view(np.uint16).reshape(8, 64, T)
    out = np.zeros((128, 8, T), np.uint16)
    for h in range(8):
        g = h // 4
        out[64 * g:64 * g + 64, h] = q[h]
    return out.view(BF)


GQA_HEADS = [[(4 * g + i, 64 * g, 0, g, 4 * g + i, None, 0) for g in range(2) for i in range(4)]]


def run_G_global(pT, pTc):
    NKB = 2 + SEQ // 128
    blocks = [(kb, None) for kb in range(NKB)]
    nc = get_prog("Gglob", lambda: build_G("glob", TOK, 512, NKB, 8, 1, 2, 1, GQA_HEADS, lambda t: blocks, 0, False))
    maps = []
    sh = shift_mat()
    for b in range(2):
        kfull = np.concatenate([pTc[512:640, b], pT[512:640, b]], axis=1)
        vfull = np.concatenate([pTc[640:768, b], pT[640:768, b]], axis=1)
        kT = np.ascontiguousarray(kfull.reshape(1, 128, 1, NKB * 128))
        va = make_vaug(np.ascontiguousarray(vfull), NKB)[None]
        for j in range(4):
            q = gqa_q_layout(np.ascontiguousarray(pT[0:512, b, j * TOK:(j + 1) * TOK]))[None]
            maps.append({"qT": q, "kT": kT, "va": va, "shm": sh})
    res = run_bass_kernel_spmd(nc, maps, core_ids=list(range(8))).results
    yT = np.empty((512, 2, SEQ), BF)
    for core in range(8):
        yT[:, core // 4, (core % 4) * TOK:(core % 4 + 1) * TOK] = res[core]["yT"]
    return yT


def build_O(with_ctx):
    k = K()
    nc, P = k.nc, k.P
    NT = TOK + (2 * CTX if with_ctx else 0)
    yT = k.inp("yT", [D, NT], BF16)
    sgT = k.inp("sgT", [D, NT], BF16)
    x = k.inp("x", [NT, D], F32)
    grow = k.inp("grow", [128, 2, D], F32)
    w_out = k.inp("w_out", [D, D], F32)
    xo = k.outp("xo", [NT, D], F32)

    w_sb = k.sb("w_sb", [128, 8, D], BF16)
    wst = [k.sb("wst%d" % i, [128, D], F32) for i in range(2)]
    gr = k.sb("gr", [128, 2, D], F32)
    yb = [k.sb("yb%d" % i, [128, 8, 512], BF16) for i in range(2)]
    gb = [k.sb("gb%d" % i, [128, 8, 512], BF16) for i in range(2)]
    xt = [k.sb("xt%d" % i, [128, D], F32) for i in range(3)]
    tm = [k.sb("tm%d" % i, [128, D], F32) for i in range(2)]
    op_ = [k.ps("op%d" % i, [128, 512], F32) for i in range(4)]

    P.dma("sp", gr[:], grow, writes=["gr"])
    w_v = w_out.rearrange("(kc p) n -> p kc n", p=128)
    for kc in range(8):
        i = kc % 2
        P.dma("sp", wst[i][:], w_v[:, kc, :], writes=["wst%d" % i])
        P.op("dve" if i else "pool", (lambda e, i=i, kc=kc: e.tensor_copy(out=w_sb[:, kc, :], in_=wst[i][:])),
             reads=["wst%d" % i], writes=["w_sb%d" % kc])
    yv = yT.rearrange("(kc p) n -> p kc n", p=128)
    sv = sgT.rearrange("(kc p) n -> p kc n", p=128)
    nb = 0
    for t in range(NT // 512):
        bi = t % 2
        gsel = 1 if t >= TOK // 512 else 0
        P.dma("sp", yb[bi][:], yv[:, :, t * 512:(t + 1) * 512], writes=["yb%d" % bi] + ["yg%d_%d" % (bi, kc) for kc in range(8)])
        P.dma("pool", gb[bi][:], sv[:, :, t * 512:(t + 1) * 512], writes=["gb%d" % bi])
        for kc in range(8):
            P.op("pool" if kc % 2 else "dve", (lambda e, bi=bi, kc=kc: e.tensor_tensor(out=yb[bi][:, kc, :], in0=yb[bi][:, kc, :], in1=gb[bi][:, kc, :],
                                                                                      op=ALU.mult)),
                 reads=["yb%d" % bi, "gb%d" % bi], writes=["yg%d_%d" % (bi, kc)])
        for s in range(4):
            xi = nb % 3
            ti = nb % 2
            tok0 = t * 512 + s * 128
            P.dma("act", xt[xi][:], x[tok0:tok0 + 128, :], writes=["xt%d" % xi])
            for hh in range(2):
                pi = (nb * 2 + hh) % 4
                for kc in range(8):
                    P.op("pe", (lambda e, pi=pi, bi=bi, kc=kc, s=s, hh=hh: e.matmul(op_[pi][:], lhsT=yb[bi][:, kc, s * 128:(s + 1) * 128],
                                                                                    rhs=w_sb[:, kc, hh * 512:(hh + 1) * 512], start=(kc == 0), stop=(kc == 7))),
                         reads=["yg%d_%d" % (bi, kc), "w_sb%d" % kc], writes=["PS:op%d" % pi])
                P.op("dve", (lambda e, pi=pi, ti=ti, hh=hh, gsel=gsel: e.tensor_tensor(out=tm[ti][:, hh * 512:(hh + 1) * 512], in0=op_[pi][:],
                                                                                        in1=gr[:, gsel, hh * 512:(hh + 1) * 512], op=ALU.mult)),
                     reads=["PS:op%d" % pi, "gr"], writes=["tm%d_%d" % (ti, hh)])
            P.op("pool", (lambda e, ti=ti, xi=xi: e.tensor_tensor(out=tm[ti][:], in0=tm[ti][:], in1=xt[xi][:], op=ALU.add)),
                 reads=["tm%d_0" % ti, "tm%d_1" % ti, "xt%d" % xi], writes=["tm%d_0" % ti, "tm%d_1" % ti])
            P.dma("sp", xo[tok0:tok0 + 128, :], tm[ti][:], reads=["tm%d_0" % ti, "tm%d_1" % ti])
            nb += 1
    return k.finish()


def run_O(layer, x, xctx, yT, yTc, sgT, sgTc, mod, cmod, inp):
    with_ctx = yTc is not None
    nc = get_prog("O", build_O, with_ctx)
    w_out = np.ascontiguousarray(inp['w_out'][layer])
    maps = []
    for core in range(8):
        b, j = core // 4, core % 4
        gate = mod[b][:, 16:24].T.reshape(-1)
        cgate = cmod[:, 16:24].T.reshape(-1)
        grow = np.ascontiguousarray(np.broadcast_to(np.stack([gate, cgate])[None], (128, 2, D))).astype(np.float32)
        sl = slice(j * TOK, (j + 1) * TOK)
        if with_ctx:
            y_ = np.concatenate([yT[:, b, sl], yTc.reshape(D, 2 * CTX)], axis=1)
            s_ = np.concatenate([sgT[:, b, sl], sgTc.reshape(D, 2 * CTX)], axis=1)
            x_ = np.concatenate([x[b, sl], xctx.reshape(2 * CTX, D)], axis=0)
        else:
            y_, s_, x_ = yT[:, b, sl], sgT[:, b, sl], x[b, sl]
        maps.append({"yT": np.ascontiguousarray(y_), "sgT": np.ascontiguousarray(s_), "x": np.ascontiguousarray(x_), "grow": grow, "w_out": w_out})
    res = run_bass_kernel_spmd(nc, maps, core_ids=list(range(8))).results
    xn = np.empty((2, SEQ, D), np.float32)
    for core in range(8):
        xn[core // 4, (core % 4) * TOK:(core % 4 + 1) * TOK] = res[core]["xo"][:TOK]
    xcn = res[0]["xo"][TOK:].reshape(2, CTX, D).copy() if with_ctx else None
    return xn, xcn


MAGIC = 12582912.0
TWO_PI = 6.283185307179586
PI_LO = 3.1415925


def build_H(n):
    k = K()
    nc, P = k.nc, k.P
    A = n // 128
    CH = min(512, n)
    NCHK = n // CH
    PIECE = min(1024, n)
    NPIECE = n // PIECE
    uT = k.inp("uT", [2, 96, 2, n], BF16)
    cwb = k.inp("cwb", [2, 96, 4], F32)
    ZT = k.inp("ZT", [2, 33, n], F32)
    tpos = k.inp("tpos", [2, 1, n], F32)
    w1 = k.inp("w1", [33, 64], F32)
    w2 = k.inp("w2", [64, 64], F32)
    bfq = k.inp("bfq", [64, 3], F32)
    w3c = k.inp("w3c", [64, 2, 128], F32)
    ndl = k.inp("ndl", [1, 128], F32)
    skr = k.inp("skr", [128, 128], F32)
    cf = k.inp("cf", [128, 2, 128], F32)
    cb16 = k.inp("cb16", [128, 2, 128], BF16)
    ybt = k.outp("ybt", [2, 128, 32, 2, A], BF16)
    Hs_t = nc.dram_tensor("Hs", [128, 2 * n], BF16, kind="Internal")
    Hs = Hs_t.ap()

    cfs = k.sb("cfs", [128, 2, 128], F32)
    idb2 = k.sb("idb", [128, 2, 128], BF16)
    idb = idb2[:, 0, :]
    jrev = idb2[:, 1, :]
    zr = [k.sb("zr%d" % i, [128, 2, A], BF16) for i in range(2)]
    zts = [k.sb("zts%d" % i, [33, CH], F32) for i in range(2)]
    tps = [k.sb("tps%d" % i, [1, CH], F32) for i in range(2)]
    w1s = k.sb("w1s", [33, 64], F32)
    w2s = k.sb("w2s", [64, 64], F32)
    bfs = k.sb("bfs", [64, 3], F32)
    w3s = k.sb("w3s", [64, 2, 128], F32)
    nds = k.sb("nds", [1, 128], F32)
    sks = k.sb("sks", [128, 128], F32)
    a1 = [k.sb("a1_%d" % i, [64, CH], F32) for i in range(2)]
    k1 = [k.sb("k1_%d" % i, [64, CH], F32) for i in range(2)]
    hd1 = [k.sb("hd1_%d" % i, [64, CH], F32) for i in range(2)]
    hd2 = [k.sb("hd2_%d" % i, [64, CH], F32) for i in range(2)]
    dec = [k.sb("dec%d" % i, [128, CH], F32) for i in range(2)]
    tb = [k.sb("tb%d" % i, [128, CH], BF16) for i in range(2)]
    jk = k.sb("jk", [128, CH], BF16)
    ssp = k.sb("ssp", [128, 2 * NCHK], F32)
    ssq = k.sb("ssq", [128, 1], F32)
    dg = k.sb("dg", [128, 128], F32)
    Srep = k.sb("Srep", [128, 128], F32)
    cws = k.sb("cws", [96, 4], F32)
    ut = [k.sb("ut%d" % i, [96, PIECE + 2], BF16) for i in range(2)]
    ac = [k.sb("ac%d" % i, [96, PIECE], F32) for i in range(2)]
    ucv = [k.sb("ucv%d" % i, [96, PIECE], BF16) for i in range(2)]
    Zt = k.sb("Zt", [128, 3, 32, 2, A], BF16)
    Gs = [k.sb("Gs%d" % i, [128, n], BF16) for i in range(3)]
    e1 = [k.sb("e1_%d" % i, [128, 2, A], F32) for i in range(2)]
    e2 = [k.sb("e2_%d" % i, [128, 2, A], F32) for i in range(2)]
    pA = k.ps("pA", [128, 512], F32)
    pB = k.ps("pB", [128, 512], F32)
    pC = k.ps("pC", [128, 512], F32)
    pD = k.ps("pD", [128, 512], F32)
    tpz = [k.ps("tpz%d" % i, [128, 1024], BF16) for i in range(2)]
    Yp = [k.ps("Yp%d" % i, [128, 512], F32) for i in range(2)]

    P.dma("sp", cfs[:], cf, writes=["cfs"])
    P.dma("sp", idb2[:], cb16, writes=["idb"])
    P.dma("sp", w1s[:], w1, writes=["w1s"])
    P.dma("sp", w2s[:], w2, writes=["w2s"])
    P.dma("sp", bfs[:], bfq, writes=["bfs"])
    P.dma("sp", w3s[:], w3c, writes=["w3s"])
    P.dma("sp", nds[:], ndl, writes=["nds"])
    P.dma("sp", sks[:], skr, writes=["sks"])

    def sin_layer(ps_t, ptag, bcol, out_t, otag, i):
        P.op("dve", (lambda e: e.tensor_scalar(out=a1[i][:], in0=ps_t, scalar1=bfs[:, bcol:bcol + 1], scalar2=bfs[:, 2:3], op0=ALU.add, op1=ALU.mult)),
             reads=[ptag, "bfs"], writes=["a1_%d" % i])
        P.op("pool", (lambda e: e.tensor_scalar(out=k1[i][:], in0=a1[i][:], scalar1=1.0 / TWO_PI, scalar2=MAGIC, op0=ALU.mult, op1=ALU.add)),
             reads=["a1_%d" % i], writes=["k1_%d" % i])
        P.op("pool", (lambda e: e.tensor_scalar(out=k1[i][:], in0=k1[i][:], scalar1=-MAGIC, scalar2=-TWO_PI, op0=ALU.add, op1=ALU.mult)),
             reads=["k1_%d" % i], writes=["k1_%d" % i])
        P.op("dve", (lambda e: e.tensor_tensor(out=a1[i][:], in0=a1[i][:], in1=k1[i][:], op=ALU.add)),
             reads=["a1_%d" % i, "k1_%d" % i], writes=["a1_%d" % i])
        P.op("pool", (lambda e: e.tensor_scalar(out=a1[i][:], in0=a1[i][:], scalar1=-PI_LO, scalar2=PI_LO, op0=ALU.max, op1=ALU.min)),
             reads=["a1_%d" % i], writes=["a1_%d" % i])
        P.op("act", (lambda e: e.activation(out=out_t[:], in_=a1[i][:], func=AF.Sin)), reads=["a1_%d" % i], writes=[otag])

    for dr in range(2):
        for c in range(NCHK):
            i = c % 2
            sl = slice(c * CH, (c + 1) * CH)
            P.dma("act", zts[i][:], ZT[dr, :, sl], writes=["zts%d" % i])
            P.dma("act", tps[i][:], tpos[dr, :, sl], writes=["tps%d" % i])
            P.op("pe", (lambda e, i=i: e.matmul(pA[0:64, 0:CH], lhsT=w1s[:], rhs=zts[i][:], start=True, stop=True)),
                 reads=["w1s", "zts%d" % i], writes=["PS:pA"])
            sin_layer(pA[0:64, 0:CH], "PS:pA", 0, hd1[i], "hd1_%d" % i, i)
            P.op("pe", (lambda e, i=i: e.matmul(pB[0:64, 0:CH], lhsT=w2s[:], rhs=hd1[i][:], start=True, stop=True)),
                 reads=["w2s", "hd1_%d" % i], writes=["PS:pB"])
            sin_layer(pB[0:64, 0:CH], "PS:pB", 1, hd2[i], "hd2_%d" % i, i)
            P.op("pe", (lambda e, i=i, dr=dr: e.matmul(pC[:, 0:CH], lhsT=w3s[:, dr, :], rhs=hd2[i][:], start=True, stop=True)),
                 reads=["w3s", "hd2_%d" % i], writes=["PS:pC"])
            P.op("pe", (lambda e, i=i: e.matmul(pD[:, 0:CH], lhsT=nds[:], rhs=tps[i][:], start=True, stop=True)),
                 reads=["nds", "tps%d" % i], writes=["PS:pD"])
            P.op("act", (lambda e, i=i: e.activation(out=dec[i][:], in_=pD[:, 0:CH], func=AF.Exp)), reads=["PS:pD"], writes=["dec%d" % i])
            P.op("dve", (lambda e, i=i: e.tensor_tensor(out=tb[i][:], in0=pC[:, 0:CH], in1=dec[i][:], op=ALU.mult)),
                 reads=["PS:pC", "dec%d" % i], writes=["tb%d" % i])
            lo = 1 if (dr == 1 and c == 0) else 0
            col = dr * NCHK + c
            P.op("act", (lambda e, i=i, lo=lo, col=col: e.activation(out=jk[:, lo:CH], in_=tb[i][:, lo:CH], func=AF.Square, accum_out=ssp[:, col:col + 1])),
                 reads=["tb%d" % i], writes=["jk", "ssp"])
            off = n if dr == 0 else 0
            P.dma("sp", Hs[:, off + c * CH: off + (c + 1) * CH], tb[i][:], reads=["tb%d" % i], writes=["Hs"])
    P.op("dve", lambda e: e.tensor_reduce(out=ssq[:], in_=ssp[:], axis=AX.X, op=ALU.add), reads=["ssp"], writes=["ssq"])
    P.op("act", lambda e: e.activation(out=ssq[:], in_=ssq[:], func=AF.Sqrt, bias=EPS), reads=["ssq"], writes=["ssq"])
    P.op("dve", lambda e: e.reciprocal(out=ssq[:], in_=ssq[:]), reads=["ssq"], writes=["ssq"])
    P.op("dve", lambda e: e.tensor_scalar(out=dg[:], in0=cfs[:, 0, :], scalar1=ssq[:], scalar2=None, op0=ALU.mult), reads=["cfs", "ssq"], writes=["dg"])
    P.op("pe", lambda e: e.matmul(pC[:, 0:128], lhsT=cfs[:, 1, :], rhs=dg[:], start=True, stop=True), reads=["cfs", "dg"], writes=["PS:pC"])
    P.op("dve", lambda e: e.tensor_copy(out=Srep[:], in_=pC[:, 0:128]), reads=["PS:pC"], writes=["Srep"])

    npc = 0
    ntr = 0
    ny = 0
    for g in range(2):
        P.dma("sp", cws[:], cwb[g], writes=["cws"])
        for b in range(2):
            for pc in range(NPIECE):
                ui = npc % 2
                t0 = pc * PIECE
                P.op("pool", (lambda e, ui=ui: e.memset(ut[ui][:, 0:1], 0.0)), writes=["ut%d" % ui])
                P.op("pool", (lambda e, ui=ui: e.memset(ut[ui][:, PIECE + 1:PIECE + 2], 0.0)), writes=["ut%d" % ui])
                lo = t0 - 1 if pc > 0 else t0
                hi = t0 + PIECE + 1 if pc < NPIECE - 1 else t0 + PIECE
                P.dma("sp", ut[ui][:, 1 + (lo - t0): 1 + (hi - t0)], uT[g, :, b, lo:hi], writes=["ut%d" % ui])
                P.op("dve", (lambda e, ui=ui: e.tensor_scalar(out=ac[ui][:], in0=ut[ui][:, 1:PIECE + 1], scalar1=cws[:, 1:2], scalar2=cws[:, 3:4],
                                                            op0=ALU.mult, op1=ALU.add)),
                     reads=["ut%d" % ui, "cws"], writes=["ac%d" % ui])
                P.op("dve", (lambda e, ui=ui: e.scalar_tensor_tensor(out=ac[ui][:], in0=ut[ui][:, 0:PIECE], scalar=cws[:, 0:1], in1=ac[ui][:],
                                                                   op0=ALU.mult, op1=ALU.add)),
                     reads=["ut%d" % ui, "cws", "ac%d" % ui], writes=["ac%d" % ui])
                P.op("dve", (lambda e, ui=ui: e.scalar_tensor_tensor(out=ucv[ui][:], in0=ut[ui][:, 2:PIECE + 2], scalar=cws[:, 2:3], in1=ac[ui][:],
                                                                   op0=ALU.mult, op1=ALU.add)),
                     reads=["ut%d" % ui, "cws", "ac%d" % ui], writes=["ucv%d" % ui])
                for s0 in range(0, PIECE // 128, 8):
                    ti = ntr % 2
                    ns = min(8, PIECE // 128 - s0)
                    for s in range(ns):
                        P.op("pe", (lambda e, ti=ti, s=s, s0=s0, ui=ui: e.transpose(tpz[ti][:, s * 128: s * 128 + 96],
                                                                                  ucv[ui][:, (s0 + s) * 128:(s0 + s + 1) * 128], idb[0:96, 0:96])),
                             reads=["ucv%d" % ui, "idb"], writes=["PS:tpz%d" % ti])
                    a0 = (t0 // 128) + s0
                    src = tpz[ti][:, 0:ns * 128].rearrange("p (s c) -> p c s", c=128)[:, 0:96, :]
                    dst = Zt[:, :, :, b, a0:a0 + ns].rearrange("p s c a -> p (s c) a")
                    P.op("act" if ti else "dve", (lambda e, src=src, dst=dst, ti=ti: (e.activation(out=dst, in_=src, func=AF.Identity) if ti
                                                                                       else e.tensor_copy(out=dst, in_=src))),
                         reads=["PS:tpz%d" % ti], writes=["Zt0", "Zt1", "Zt2"])
                    ntr += 1
                npc += 1
        for o in range(2):
            for c in range(32):
                row = o * 64 + g * 32 + c
                gi = ny % 2
                yp = Yp[gi]
                ytag = "PS:Yp%d" % gi
                base = Hs_t.ap()[row:row + 1, :]
                srcF = bass.AP(tensor=base.tensor, offset=base.offset + n - 127, ap=[[1, 128], [1, n]])
                srcB = bass.AP(tensor=base.tensor, offset=base.offset + 1, ap=[[1, 128], [1, n - 128]])
                fi = (2 * ny) % 3
                bi_ = (2 * ny + 1) % 3
                P.dma("sp", Gs[fi][:], srcF, reads=["Hs"], writes=["Gs%d" % fi])
                if A > 1:
                    P.dma("act", Gs[bi_][:, 0:n - 128], srcB, reads=["Hs"], writes=["Gs%d" % bi_])
                zin = Zt[:, o, c]
                yv = yp[:, 0:2 * A].rearrange("p (b a) -> p b a", b=2)
                pDv = pD[:, 0:2 * A].rearrange("p (b a) -> p b a", b=2)
                P.op("pe", (lambda e, zin=zin, pDv=pDv: e.matmul(pDv, lhsT=jrev, rhs=zin, start=True, stop=True)),
                     reads=["idb", "Zt%d" % o], writes=["PS:pD"])
                P.op("act", (lambda e, gi=gi, pDv=pDv: e.activation(out=zr[gi][:], in_=pDv, func=AF.Identity)),
                     reads=["PS:pD"], writes=["zr%d" % gi])
                zrv = zr[gi]
                nmm = 2 * A - 1
                im = 0
                for d in range(A):
                    P.op("pe", (lambda e, d=d, fi=fi, yv=yv, zrv=zrv, im=im: e.matmul(yv[:, :, d:A], lhsT=Gs[fi][:, d * 128:(d + 1) * 128],
                                                                                     rhs=zrv[:, :, 0:A - d], start=(im == 0), stop=(im == nmm - 1))),
                         reads=["Gs%d" % fi, "zr%d" % gi], writes=[ytag])
                    im += 1
                for ee in range(1, A):
                    P.op("pe", (lambda e, ee=ee, bi_=bi_, yv=yv, zrv=zrv, im=im: e.matmul(yv[:, :, 0:A - ee], lhsT=Gs[bi_][:, (A - 1 - ee) * 128:(A - ee) * 128],
                                                                                 rhs=zrv[:, :, ee:A], start=False, stop=(im == nmm - 1))),
                         reads=["Gs%d" % bi_, "zr%d" % gi], writes=[ytag])
                    im += 1
                col = o * 64 + g * 32 + c
                P.op("act", (lambda e, gi=gi, yv=yv, col=col: e.activation(out=e1[gi][:], in_=yv, func=AF.Identity, scale=Srep[:, col:col + 1])),
                     reads=[ytag, "Srep"], writes=["e1_%d" % gi])
                P.op("dve", (lambda e, gi=gi, zin=zin, col=col: e.scalar_tensor_tensor(out=e2[gi][:], in0=zin, scalar=sks[:, col:col + 1], in1=e1[gi][:],
                                                                                       op0=ALU.mult, op1=ALU.add)),
                     reads=["Zt%d" % o, "sks", "e1_%d" % gi], writes=["e2_%d" % gi])
                P.op("pool", (lambda e, gi=gi, o=o, c=c: e.tensor_tensor(out=Zt[:, o + 1, c], in0=Zt[:, o + 1, c], in1=e2[gi][:], op=ALU.mult)),
                     reads=["Zt%d" % (o + 1), "e2_%d" % gi], writes=["Zt%d" % (o + 1)])
                ny += 1
        P.dma("sp", ybt[g], Zt[:, 2], reads=["Zt2"])
    return k.finish()


def hyena_consts(n):
    m_f = np.arange(n)
    m_r = np.concatenate([[0], n - np.arange(1, n)])
    t_lin = np.linspace(0.0, 1.0, n, dtype=np.float32)
    f = np.linspace(1e-4, 15.0, 16, dtype=np.float32)[None, :]
    ZT = np.empty((2, 33, n), np.float32)
    tpos = np.empty((2, 1, n), np.float32)
    for i, m in enumerate((m_f, m_r)):
        w = (np.float32(2.0 * np.pi) * m.astype(np.float32)[:, None] / np.float32(n)).astype(np.float32)
        fw = (f * w).astype(np.float32)
        z = np.concatenate([t_lin[m][:, None], np.cos(fw), -np.sin(fw)], axis=-1).astype(np.float32)
        ZT[i] = z.T
        tpos[i, 0] = t_lin[m]
    deltas = np.linspace(np.log(1e-2) / 1.5, np.log(1e-2) / 0.3, 512, dtype=np.float32)
    return ZT, tpos, np.abs(deltas).astype(np.float32)


def run_H(n, uT_all, inp):
    nc = get_prog("H", build_H, n)
    A = n // 128
    ZT, tpos, adel = hyena_consts(n)
    cw = inp['hy_conv_w'][0]
    cbias = inp['hy_conv_b'][0]
    w3 = inp['hy_w3'][0]
    skip = inp['hy_skip'][0]
    cf = np.zeros((128, 2, 128), np.float32)
    cf[:, 0, :] = np.eye(128)
    cf[:, 1, :] = 1.0
    cb16 = np.ascontiguousarray(np.stack([np.eye(128, dtype=np.float32), np.eye(128, dtype=np.float32)[::-1]], axis=1)).astype(BF)
    bfq = np.ascontiguousarray(np.stack([inp['hy_b1'][0], inp['hy_b2'][0], inp['hy_freq'][0]], axis=1).astype(np.float32))
    u16 = uT_all.view(np.uint16)
    maps = []
    for core in range(8):
        ch0 = 64 * core
        uT = np.empty((2, 96, 2, n), np.uint16)
        cwb = np.empty((2, 96, 4), np.float32)
        for g in range(2):
            for s in range(3):
                r0 = s * 512 + ch0 + g * 32
                uT[g, s * 32:(s + 1) * 32] = u16[r0:r0 + 32]
                cwb[g, s * 32:(s + 1) * 32, 0:3] = cw[:, r0:r0 + 32].T
                cwb[g, s * 32:(s + 1) * 32, 3] = cbias[r0:r0 + 32]
        w3c = np.empty((64, 2, 128), np.float32)
        for dr in range(2):
            for o in range(2):
                c0 = o * 1024 + dr * 512 + ch0
                w3c[:, dr, o * 64:(o + 1) * 64] = w3[:, c0:c0 + 64]
        ndl = -np.tile(adel[ch0:ch0 + 64], 2)[None, :].astype(np.float32)
        skr = np.ascontiguousarray(np.broadcast_to(np.concatenate([skip[0, ch0:ch0 + 64], skip[1, ch0:ch0 + 64]])[None, :], (128, 128))).astype(np.float32)
        maps.append({"uT": uT.view(BF), "cwb": cwb, "ZT": ZT, "tpos": tpos, "w1": np.ascontiguousarray(inp['hy_w1'][0]),
                     "w2": np.ascontiguousarray(inp['hy_w2'][0]), "bfq": bfq, "w3c": w3c, "ndl": ndl, "skr": skr, "cf": cf, "cb16": cb16})
    res = run_bass_kernel_spmd(nc, maps, core_ids=list(range(8))).results
    ybT = np.empty((512, 2, n), np.uint16)
    for core in range(8):
        o = res[core]["ybt"].view(np.uint16)
        ybT[64 * core:64 * core + 64] = o.transpose(0, 2, 3, 4, 1).reshape(64, 2, n)
    return ybT.view(BF)


def run_G_ctx(pTc):
    blocks = [(0, None), (1, None)]
    nc = get_prog("Gctx", lambda: build_G("ctx", CTX, 128, 2, 8, 1, 2, 1, GQA_HEADS, lambda t: blocks, 0, False))
    maps = []
    sh = shift_mat()
    for b in range(2):
        kT = np.ascontiguousarray(pTc[512:640, b]).reshape(1, 128, 1, CTX)
        va = make_vaug(np.ascontiguousarray(pTc[640:768, b]), 2)[None]
        q = gqa_q_layout(np.ascontiguousarray(pTc[0:512, b]))[None]
        maps.append({"qT": q, "kT": kT, "va": va, "shm": sh})
    res = run_bass_kernel_spmd(nc, maps, core_ids=[0, 1]).results
    return np.ascontiguousarray(np.stack([res[0]["yT"], res[1]["yT"]], axis=1))


def halo_rows(rows_main, rows_ctx, j, H):
    R = rows_main.shape[0]
    out = np.zeros((R, CTX + (2 * H + 32) * 128), np.uint16)
    out[:, :CTX] = rows_ctx
    lo = j * TOK - 128 * H
    hi = (j + 1) * TOK + 128 * H
    slo, shi = max(lo, 0), min(hi, SEQ)
    out[:, CTX + (slo - lo): CTX + (shi - lo)] = rows_main[:, slo:shi]
    return out


NEGB = -30000.0


def win_tables(j):
    B = np.full((128, 9, 128), NEGB, np.float32)
    r = np.arange(128)[:, None]
    cq = np.arange(128)[None, :]
    for v, t in enumerate((0, 1, 31)):
        gi = 32 * j + t
        for o in range(3):
            gk = gi + o - 1
            if 0 <= gk < SEQ // 128:
                ok = np.abs(128 * (o - 1) + r - cq) <= 128
                B[:, v * 3 + o, :] = np.where(ok, 0.0, NEGB)
    return B


def run_G_win(pT, pTc, inp):
    NKB = 2 + 34

    def blocks(t):
        v = 0 if t == 0 else (2 if t == 31 else 1)
        return [(0, None), (1, None)] + [(2 + t + o, (lambda bh, v=v, o=o: v * 3 + o)) for o in range(3)]
    heads = [[(4 * g + i, 64 * g, 0, g, 4 * g + i, 4 * g + i, 0) for g in range(2) for i in range(4)]]
    nc = get_prog("Gwin", lambda: build_G("win", TOK, 128, NKB, 8, 1, 2, 1, heads, blocks, 9, True))
    sh = shift_mat()
    snk = np.ascontiguousarray(np.broadcast_to(np.asarray(inp['win_sink'][0], np.float32)[None, :], (128, 8)))
    p16, c16 = pT.view(np.uint16), pTc.view(np.uint16)
    maps = []
    for core in range(8):
        b, j = core // 4, core % 4
        kk = halo_rows(p16[512:640, b], c16[512:640, b], j, 1)
        vv = halo_rows(p16[640:768, b], c16[640:768, b], j, 1)
        q = gqa_q_layout(np.ascontiguousarray(pT[0:512, b, j * TOK:(j + 1) * TOK]))[None]
        maps.append({"qT": q, "kT": kk.view(BF).reshape(1, 128, 1, NKB * 128), "va": make_vaug(vv.view(BF), NKB)[None], "shm": sh,
                     "Bt": win_tables(j), "snk": snk})
    res = run_bass_kernel_spmd(nc, maps, core_ids=list(range(8))).results
    yT = np.empty((512, 2, SEQ), BF)
    for core in range(8):
        yT[:, core // 4, (core % 4) * TOK:(core % 4 + 1) * TOK] = res[core]["yT"]
    return yT


def na_tables(j, rpb):
    B = np.full((128, 216, 128), NEGB, np.float32)
    r = np.arange(128)[:, None]
    cq = np.arange(128)[None, :]
    for v, t in enumerate((0, 1, 2, 30, 31)):
        gi = 32 * j + t
        qr = 2 * gi + cq // 64
        qc = cq % 64
        r0 = np.clip(qr - 4, 0, 256 - 8)
        c0 = np.clip(qc - 8, 0, 64 - 16)
        olist = list(range(5)) + ([5] if v == 0 else []) + ([-1] if v == 4 else [])
        for o in olist:
            gk = gi + o - 2
            if not (0 <= gk < SEQ // 128):
                continue
            kr = 2 * gk + r // 64
            kc = r % 64
            ok = (kr >= r0) & (kr < r0 + 8) & (kc >= c0) & (kc < c0 + 16)
            ri = np.clip(kr - qr + 7, 0, 14)
            ci = np.clip(kc - qc + 15, 0, 30)
            for h in range(8):
                idx = (v * 8 + h) * 5 + o if 0 <= o < 5 else (200 + h if o == 5 else 208 + h)
                B[:, idx, :] = np.where(ok, rpb[h][ri, ci], NEGB)
    return B


def run_G_na(pT, pTc, inp):
    NKB = 2 + 36

    def blocks(t):
        v = {0: 0, 1: 1, 30: 3, 31: 4}.get(t, 2)
        bl = [(0, None), (1, None)] + [(2 + t + o, (lambda bh, v=v, o=o: (v * 8 + bh) * 5 + o)) for o in range(5)]
        if t == 0:
            bl.append((2 + t + 5, (lambda bh: 200 + bh)))
        if t == 31:
            bl.append((2 + t - 1, (lambda bh: 208 + bh)))
        return bl
    heads = [[(h % 4, 64 * (h % 2), (h // 2) % 2, h % 4, h, None, h) for h in range(4 * p, 4 * p + 4)] for p in range(2)]
    nc = get_prog("Gna", lambda: build_G("na", TOK, 128, NKB, 4, 2, 4, 2, heads, blocks, 216, False))
    sh = shift_mat()
    rpb = np.asarray(inp['nat_rpb'][0], np.float32)
    p16, c16 = pT.view(np.uint16), pTc.view(np.uint16)
    maps = []
    for core in range(8):
        b, j = core // 4, core % 4
        qs, ks, vs = [], [], []
        for p in range(2):
            q = p16[768 + 256 * p: 768 + 256 * (p + 1), b, j * TOK:(j + 1) * TOK].reshape(4, 64, TOK)
            qz = np.zeros((128, 4, TOK), np.uint16)
            for hh in range(4):
                qz[64 * (hh % 2):64 * (hh % 2) + 64, hh] = q[hh]
            qs.append(qz)
            kk = halo_rows(p16[1280 + 256 * p: 1280 + 256 * (p + 1), b], c16[1280 + 256 * p: 1280 + 256 * (p + 1), b], j, 2)
            ks.append(np.ascontiguousarray(kk.reshape(2, 128, NKB * 128).transpose(1, 0, 2)))
            vv = halo_rows(p16[1792 + 256 * p: 1792 + 256 * (p + 1), b], c16[1792 + 256 * p: 1792 + 256 * (p + 1), b], j, 2)
            vs.append(make_vaug(vv.view(BF), NKB))
        maps.append({"qT": np.stack(qs).view(BF), "kT": np.stack(ks).view(BF), "va": np.stack(vs), "shm": sh, "Bt": na_tables(j, rpb)})
    res = run_bass_kernel_spmd(nc, maps, core_ids=list(range(8))).results
    yT = np.empty((512, 2, SEQ), BF)
    for core in range(8):
        yT[:, core // 4, (core % 4) * TOK:(core % 4 + 1) * TOK] = res[core]["yT"]
    return yT


def kernel(**inp):
    inp = {k_: np.asarray(v) for k_, v in inp.items()}
    x = np.ascontiguousarray(inp['x'], dtype=np.float32)
    xctx = np.ascontiguousarray(inp['ctx'], dtype=np.float32)
    pT, pTc, mod, cmod = run_P(0, x, xctx, inp)
    ya = run_G_global(pT, pTc)
    yca = run_G_ctx(pTc)
    yb = run_H(SEQ, np.ascontiguousarray(pT[768:2304]), inp)
    ycb = run_H(CTX, np.ascontiguousarray(pTc[768:2304]), inp)
    yT = np.concatenate([ya, yb], axis=0)
    yTc = np.concatenate([yca, ycb], axis=0)
    x, xctx = run_O(0, x, xctx, yT, yTc, pT[2304:3328], pTc[2304:3328], mod, cmod, inp)
    pT, pTc, mod, cmod = run_P(1, x, xctx, inp)
    yw = run_G_win(pT, pTc, inp)
    yn = run_G_na(pT, pTc, inp)
    yT = np.concatenate([yw, yn], axis=0)
    x, _ = run_O(1, x, xctx, yT, None, pT[2304:3328], None, mod, cmod, inp)
    return x.astype(np.float32)
```

```python
import contextlib
import numpy as np
import ml_dtypes
import concourse.bass as bass
import concourse.mybir as mybir
from concourse.bass_utils import run_bass_kernel_spmd

F32 = mybir.dt.float32
BF16 = mybir.dt.bfloat16
AF = mybir.ActivationFunctionType
ALU = mybir.AluOpType
AX = mybir.AxisListType

COMPUTE = ("pe", "act", "dve", "pool")
NRING = 24
NRING_SW = 8


class Op:
    __slots__ = ("eng", "fn", "reads", "writes", "dma", "deps", "sig", "cnt", "sem", "val")

    def __init__(self, eng, fn, reads, writes, dma):
        self.eng = eng
        self.fn = fn
        self.reads = tuple(reads)
        self.writes = tuple(writes)
        self.dma = dma
        self.deps = ()
        self.sig = False
        self.cnt = 0
        self.sem = None
        self.val = 0


class Prog:
    def __init__(self, nc, same_engine_sync=True):
        self.nc = nc
        self.ops = []
        self.same_engine_sync = same_engine_sync

    def op(self, eng, fn, reads=(), writes=()):
        self.ops.append(Op(eng, fn, reads, writes, False))

    def dma(self, q, out, in_, reads=(), writes=(), **kw):
        self.ops.append(Op(q, lambda e: e.dma_start(out=out, in_=in_, **kw), reads, writes, True))

    def analyze(self):
        last_w = {}
        readers = {}
        bank_last = {}
        ndma = 0
        nd = {"hw": 0, "sw": 0}
        dma_ops = {"hw": [], "sw": []}
        for i, o in enumerate(self.ops):
            deps = set()
            for t in o.reads:
                if t in last_w:
                    deps.add(last_w[t])
            for t in o.writes:
                if t in last_w:
                    deps.add(last_w[t])
                for r in readers.get(t, ()):
                    deps.add(r)
            for t in set(o.reads) | set(o.writes):
                if isinstance(t, str) and t.startswith("PS:"):
                    bl = bank_last.setdefault(t, {})
                    for e2, j2 in bl.items():
                        if e2 != o.eng:
                            deps.add(j2)
                    bl[o.eng] = i
            if o.dma:
                cls = "sw" if o.eng == "pool" else "hw"
                R_, b_ = (NRING_SW, NRING) if cls == "sw" else (NRING, 0)
                if nd[cls] >= R_:
                    deps.add(dma_ops[cls][nd[cls] - R_])
                dma_ops[cls].append(i)
                o.sem = b_ + nd[cls] % R_
                o.val = 16 * (nd[cls] // R_ + 1)
                nd[cls] += 1
                ndma += 1
            deps.discard(i)
            keep = []
            for j in deps:
                oj = self.ops[j]
                if (not oj.dma) and oj.eng == o.eng and (not o.dma or True):
                    if oj.eng == "pe" or not self.same_engine_sync:
                        continue
                keep.append(j)
                if not oj.dma:
                    oj.sig = True
            o.deps = tuple(sorted(keep))
            for t in o.reads:
                readers.setdefault(t, []).append(i)
            for t in o.writes:
                last_w[t] = i
                readers[t] = []
        cnt = {e: 0 for e in COMPUTE + ("sp",)}
        for o in self.ops:
            if not o.dma and o.sig:
                cnt[o.eng] += 1
                o.cnt = cnt[o.eng]
        self.ndma = ndma
        self.final_cnt = cnt

    def emit(self, stack):
        nc = self.nc
        self.analyze()
        esem = {e: stack.enter_context(nc.semaphore("s_" + e)) for e in COMPUTE + ("sp",)}
        ring = [stack.enter_context(nc.semaphore("d%d" % i)) for i in range(NRING + NRING_SW)]
        block = stack.enter_context(nc.Block())
        ops = self.ops
        ring_final = {}
        for o in ops:
            if o.dma:
                ring_final[o.sem] = max(ring_final.get(o.sem, 0), o.val)

        def stream(engname, eng):
            waited = {}
            for o in ops:
                if o.eng != engname:
                    continue
                for j in o.deps:
                    oj = ops[j]
                    if oj.dma:
                        s, v, key = ring[oj.sem], oj.val, ("d", oj.sem)
                    else:
                        s, v, key = esem[oj.eng], oj.cnt, ("e", oj.eng)
                    if waited.get(key, 0) >= v:
                        continue
                    waited[key] = v
                    eng.wait_ge(s, v)
                ins = o.fn(eng)
                if o.dma:
                    ins.then_inc(ring[o.sem], 16)
                elif o.sig:
                    ins.then_inc(esem[o.eng], 1)
            if engname == "sp":
                for s, v in sorted(ring_final.items()):
                    if waited.get(("d", s), 0) < v:
                        eng.wait_ge(ring[s], v)

        @block.sync
        def _(e):
            stream("sp", e)

        @block.tensor
        def _(e):
            stream("pe", e)

        @block.scalar
        def _(e):
            stream("act", e)

        @block.vector
        def _(e):
            stream("dve", e)

        @block.gpsimd
        def _(e):
            stream("pool", e)


D = 1024
SEQ = 16384
NB = 2
CTX = 256
INW = 3328
NCH = INW // 128
TOK = 4096
EPS = 1e-6
BF = ml_dtypes.bfloat16

CH_TYPES = [
    ['rope'] * 5 + ['plain'] * 13 + ['silu'] * 8,
    ['rope'] * 5 + ['plain'] + ['norm'] * 8 + ['plain'] * 4 + ['silu'] * 8,
]


class K:
    def __init__(self):
        self.nc = bass.Bass("TRN2", target_bir_lowering=False)
        self.st = contextlib.ExitStack()
        self.P = Prog(self.nc)

    def inp(self, name, shape, dt):
        return self.nc.dram_tensor(name, list(shape), dt, kind="ExternalInput").ap()

    def outp(self, name, shape, dt):
        return self.nc.dram_tensor(name, list(shape), dt, kind="ExternalOutput").ap()

    def sb(self, name, shape, dt):
        return self.st.enter_context(self.nc.sbuf_tensor(name, list(shape), dt))

    def ps(self, name, shape, dt):
        return self.st.enter_context(self.nc.psum_tensor(name, list(shape), dt))

    def finish(self):
        self.P.emit(self.st)
        self.st.close()
        return self.nc


def build_P(layer):
    k = K()
    nc, P = k.nc, k.P
    types = CH_TYPES[layer]
    x = k.inp("x", [TOK, D], F32)
    cx = k.inp("cx", [2 * CTX, D], F32)
    cT = k.inp("cT", [128, 8, 2], F32)
    w_ada = k.inp("w_ada", [D, 3 * D], F32)
    b_ada = k.inp("b_ada", [128, 24, 2], F32)
    ng = k.inp("ng", [128, 8], F32)
    w_in = k.inp("w_in", [D, INW], F32)
    gains = k.inp("gains", [128, NCH], F32)
    cosT = k.inp("cosT", [128, TOK], F32)
    sinT = k.inp("sinT", [128, TOK], F32)
    consts = k.inp("consts", [128, 3, 128], BF16)
    pT = k.outp("pT", [INW, TOK + 2 * CTX], BF16)
    modo = k.outp("modo", [128, 24, 2], F32)

    cst = k.sb("cst", [128, 3, 128], BF16)
    sc = k.sb("sc", [128, 8, 2], F32)
    wa = [k.sb("wa%d" % i, [128, 8, 128], F32) for i in range(2)]
    bad = k.sb("bad", [128, 24, 2], F32)
    mod = k.sb("mod", [128, 24, 2], F32)
    ngs = k.sb("ngs", [128, 8], F32)
    Asc = k.sb("Asc", [128, 8, 2], F32)
    gsb = k.sb("gsb", [128, NCH], F32)
    w_sb = k.sb("w_sb", [128, 8, INW], BF16)
    wst = [k.sb("wst%d" % i, [128, INW // 2], F32) for i in range(2)]
    cs = k.sb("cs", [128, TOK], F32)
    sn = k.sb("sn", [128, TOK], F32)
    xt = [k.sb("xt%d" % i, [128, D], F32) for i in range(3)]
    xn = [k.sb("xn%d" % i, [128, D], BF16) for i in range(2)]
    junk = k.sb("junk", [128, D], BF16)
    ssq = [k.sb("ssq%d" % i, [128, 1], F32) for i in range(3)]
    rstd = [k.sb("rstd%d" % i, [128, 1], F32) for i in range(3)]
    hT = [k.sb("hT%d" % i, [128, 8, 512], BF16) for i in range(2)]
    sq = [k.sb("sq%d" % i, [128, 512], BF16) for i in range(2)]
    rs = [k.sb("rs%d" % i, [128, 512], F32) for i in range(2)]
    qnb = [k.sb("qnb%d" % i, [128, 512], BF16) for i in range(2)]
    t1 = [k.sb("t1%d" % i, [128, 512], F32) for i in range(2)]
    t2 = [k.sb("t2%d" % i, [128, 512], F32) for i in range(2)]
    oT = [k.sb("oT%d" % i, [128, 512], BF16) for i in range(3)]
    mod_ps_full = k.ps("mod_ps", [128, 512], F32)
    mod_ps = mod_ps_full[:, 0:48].rearrange("p (a b) -> p a b", b=2)
    tp = [k.ps("tp%d" % i, [128, 8, 128], BF16) for i in range(2)]
    pp = [k.ps("pp%d" % i, [128, 512], F32) for i in range(2)]
    ss = [k.ps("ss%d" % i, [128, 512], F32) for i in range(2)]

    P.dma("sp", cst[:], consts, writes=["cst"])
    P.dma("sp", sc[:], cT, writes=["sc"])
    P.dma("sp", bad[:], b_ada, writes=["bad"])
    P.dma("sp", ngs[:], ng, writes=["ngs"])
    P.dma("sp", gsb[:], gains, writes=["gsb"])
    P.dma("pool", cs[:], cosT, writes=["cs"])
    P.dma("pool", sn[:], sinT, writes=["sn"])
    ident = cst[:, 0, :]
    pswap = cst[:, 1, :]
    onesb = cst[:, 2, :]

    P.op("act", lambda e: e.activation(out=sc[:], in_=sc[:], func=AF.Silu), reads=["sc"], writes=["sc"])
    w_ada_v = w_ada.rearrange("(kc p) n -> p kc n", p=128)
    for oc in range(24):
        wt = wa[oc % 2]
        tg = "wa%d" % (oc % 2)
        P.dma("sp", wt[:], w_ada_v[:, :, oc * 128:(oc + 1) * 128], writes=[tg])
        for kc in range(8):
            P.op("pe", (lambda e, wt=wt, kc=kc, oc=oc: e.matmul(mod_ps[:, oc, :], lhsT=wt[:, kc, :], rhs=sc[:, kc, :],
                                                             start=(kc == 0), stop=(kc == 7))),
                 reads=[tg, "sc"], writes=["PS:mod_ps"])
    P.op("dve", lambda e: e.tensor_tensor(out=mod[:], in0=mod_ps[:], in1=bad[:], op=ALU.add),
         reads=["PS:mod_ps", "bad"], writes=["mod"])
    P.dma("sp", modo, mod[:], reads=["mod"])
    for j in range(2):
        P.op("dve", (lambda e, j=j: e.scalar_tensor_tensor(out=Asc[:, :, j], in0=mod[:, 8:16, j], scalar=1.0, in1=ngs[:],
                                                        op0=ALU.add, op1=ALU.mult)),
             reads=["mod", "ngs"], writes=["Asc"])

    import os
    STAGE = int(os.environ.get("PSTAGE", "9"))
    if STAGE < 2:
        return k.finish()
    w_in_v = w_in.rearrange("(kc p) n -> p kc n", p=128)
    H = INW // 2
    for kc in range(8):
        for hh in range(2):
            i = (kc * 2 + hh) % 2
            P.dma("act" if hh else "sp", wst[i][:], w_in_v[:, kc, hh * H:(hh + 1) * H], writes=["wst%d" % i])
            eng = "pool" if hh else "dve"
            P.op(eng, (lambda e, i=i, kc=kc, hh=hh: e.tensor_copy(out=w_sb[:, kc, hh * H:(hh + 1) * H], in_=wst[i][:])),
                 reads=["wst%d" % i], writes=["w_sb%d_%d" % (kc, hh)])
    WTAGS = ["w_sb%d_%d" % (kc, hh) for kc in range(8) for hh in range(2)]

    nblk = 0
    nchunk = 0
    if STAGE < 3:
        return k.finish()
    for t in range(9 if STAGE >= 9 else 1):
        is_ctx = (t == 8)
        j = 1 if is_ctx else 0
        hb = hT[t % 2]
        htag = "hT%d" % (t % 2)
        for s in range(4):
            xi = nblk % 3
            xni = nblk % 2
            tpi = nblk % 2
            src = cx[s * 128:(s + 1) * 128, :] if is_ctx else x[t * 512 + s * 128: t * 512 + (s + 1) * 128, :]
            P.dma("sp", xt[xi][:], src, writes=["xt%d" % xi])
            P.op("act", (lambda e, xi=xi: e.activation(out=junk[:], in_=xt[xi][:], func=AF.Square, accum_out=ssq[xi][:])),
                 reads=["xt%d" % xi], writes=["junk", "ssq%d" % xi])
            PSUB = int(os.environ.get("PSUB", "9"))
            if PSUB < 2:
                continue
            P.op("act", (lambda e, xi=xi: e.activation(out=rstd[xi][:], in_=ssq[xi][:], func=AF.Sqrt, scale=1.0 / D, bias=EPS)),
                 reads=["ssq%d" % xi], writes=["rstd%d" % xi])
            P.op("dve", (lambda e, xi=xi: e.reciprocal(out=rstd[xi][:], in_=rstd[xi][:])),
                 reads=["rstd%d" % xi], writes=["rstd%d" % xi])
            if PSUB < 3:
                continue
            P.op("dve", (lambda e, xi=xi, xni=xni: e.tensor_scalar(out=xn[xni][:], in0=xt[xi][:], scalar1=rstd[xi][:], scalar2=None,
                                                                 op0=ALU.mult)),
                 reads=["xt%d" % xi, "rstd%d" % xi], writes=["xn%d" % xni])
            if PSUB < 4:
                continue
            for kc in range(8):
                P.op("pe", (lambda e, kc=kc, xni=xni, tpi=tpi: e.transpose(tp[tpi][:, kc, :], xn[xni][:, kc * 128:(kc + 1) * 128], ident)),
                     reads=["xn%d" % xni, "cst"], writes=["PS:tp%d" % tpi])
            if PSUB < 5:
                continue
            for kc in range(8):
                eng = "dve" if tpi == 0 else "act"
                if eng == "dve":
                    fn = (lambda e, kc=kc, tpi=tpi, s=s, j=j, hb=hb: e.tensor_scalar(
                        out=hb[:, kc, s * 128:(s + 1) * 128], in0=tp[tpi][:, kc, :], scalar1=Asc[:, kc, j:j + 1],
                        scalar2=mod[:, kc, j:j + 1], op0=ALU.mult, op1=ALU.add))
                else:
                    fn = (lambda e, kc=kc, tpi=tpi, s=s, j=j, hb=hb: e.activation(
                        out=hb[:, kc, s * 128:(s + 1) * 128], in_=tp[tpi][:, kc, :], func=AF.Identity,
                        scale=Asc[:, kc, j:j + 1], bias=mod[:, kc, j:j + 1]))
                P.op(eng, fn, reads=["PS:tp%d" % tpi, "Asc", "mod"], writes=[htag + "_%d" % kc])
            nblk += 1
        HT = [htag + "_%d" % kc for kc in range(8)]
        tcol = (TOK + 0) if is_ctx else t * 512
        for cc in range(NCH):
            ty = types[cc]
            if STAGE == 3:
                break
            if STAGE == 4 and ty != 'plain':
                continue
            if STAGE == 5 and ty == 'rope':
                ty = 'norm'
            if is_ctx and ty == 'rope':
                ty = 'norm'
            pi = nchunk % 2
            oi = nchunk % 3
            ppt = pp[pi]
            ptag = "PS:pp%d" % pi
            for kc in range(8):
                P.op("pe", (lambda e, kc=kc, cc=cc, ppt=ppt, hb=hb: e.matmul(ppt[:], lhsT=w_sb[:, kc, cc * 128:(cc + 1) * 128],
                                                                           rhs=hb[:, kc, :], start=(kc == 0), stop=(kc == 7))),
                     reads=[HT[kc], "w_sb%d_%d" % (kc, cc // 13)], writes=[ptag])
            ob = oT[oi]
            otag = "oT%d" % oi
            if ty == 'plain':
                if cc % 2 == 0:
                    P.op("act", (lambda e, ppt=ppt, ob=ob: e.activation(out=ob[:], in_=ppt[:], func=AF.Identity)),
                         reads=[ptag], writes=[otag])
                else:
                    P.op("dve", (lambda e, ppt=ppt, ob=ob: e.tensor_copy(out=ob[:], in_=ppt[:])), reads=[ptag], writes=[otag])
            elif ty == 'silu':
                P.op("act", (lambda e, ppt=ppt, ob=ob: e.activation(out=ob[:], in_=ppt[:], func=AF.Silu)),
                     reads=[ptag], writes=[otag])
            else:
                P.op("act", (lambda e, ppt=ppt, pi=pi: e.activation(out=sq[pi][:], in_=ppt[:], func=AF.Square)),
                     reads=[ptag], writes=["sq%d" % pi])
                P.op("pe", (lambda e, pi=pi: e.matmul(ss[pi][:], lhsT=onesb, rhs=sq[pi][:], start=True, stop=True)),
                     reads=["sq%d" % pi, "cst"], writes=["PS:ss%d" % pi])
                P.op("act", (lambda e, pi=pi: e.activation(out=rs[pi][:], in_=ss[pi][:], func=AF.Sqrt, scale=1.0 / 64, bias=EPS)),
                     reads=["PS:ss%d" % pi], writes=["rs%d" % pi])
                P.op("dve", (lambda e, pi=pi: e.reciprocal(out=rs[pi][:], in_=rs[pi][:])), reads=["rs%d" % pi], writes=["rs%d" % pi])
                dst = ob if ty == 'norm' else qnb[pi]
                dtag = otag if ty == 'norm' else "qnb%d" % pi
                P.op("dve", (lambda e, ppt=ppt, pi=pi, cc=cc, dst=dst: e.scalar_tensor_tensor(
                    out=dst[:], in0=ppt[:], scalar=gsb[:, cc:cc + 1], in1=rs[pi][:], op0=ALU.mult, op1=ALU.mult)),
                    reads=[ptag, "gsb", "rs%d" % pi], writes=[dtag])
                if ty == 'rope':
                    P.op("pe", (lambda e, pi=pi: e.matmul(ss[pi][:], lhsT=pswap, rhs=qnb[pi][:], start=True, stop=True)),
                         reads=["qnb%d" % pi, "cst"], writes=["PS:ss%d" % pi])
                    P.op("pool", (lambda e, pi=pi, t=t: e.tensor_tensor(out=t1[pi][:], in0=qnb[pi][:], in1=cs[:, t * 512:(t + 1) * 512],
                                                                      op=ALU.mult)),
                         reads=["qnb%d" % pi, "cs"], writes=["t1%d" % pi])
                    P.op("dve", (lambda e, pi=pi, t=t: e.tensor_tensor(out=t2[pi][:], in0=ss[pi][:], in1=sn[:, t * 512:(t + 1) * 512],
                                                                     op=ALU.mult)),
                         reads=["PS:ss%d" % pi, "sn"], writes=["t2%d" % pi])
                    P.op("dve", (lambda e, pi=pi, ob=ob: e.tensor_tensor(out=ob[:], in0=t1[pi][:], in1=t2[pi][:], op=ALU.add)),
                         reads=["t1%d" % pi, "t2%d" % pi], writes=[otag])
            P.dma("sp" if cc % 2 else "pool", pT[cc * 128:(cc + 1) * 128, tcol:tcol + 512], ob[:], reads=[otag])
            nchunk += 1
    return k.finish()


def rope_tables():
    t = np.arange(SEQ)
    row = (t // 64).astype(np.float32)
    col = (t % 64).astype(np.float32)
    inv = (10000.0 ** (-np.arange(16, dtype=np.float32) / 16)).astype(np.float32)
    ang = np.concatenate([row[:, None] * inv, col[:, None] * inv], axis=-1).astype(np.float32)
    cos = np.cos(ang).astype(np.float32)
    sin = np.sin(ang).astype(np.float32)
    cosT = np.repeat(cos, 2, axis=1).T
    sgn = np.tile(np.array([-1.0, 1.0], np.float32), 32)[:, None]
    sinT = np.repeat(sin, 2, axis=1).T * sgn
    cosT = np.ascontiguousarray(np.concatenate([cosT, cosT], 0))
    sinT = np.ascontiguousarray(np.concatenate([sinT, sinT], 0))
    return cosT.astype(np.float32), sinT.astype(np.float32)


def make_consts():
    c = np.zeros((128, 3, 128), np.float32)
    c[:, 0, :] = np.eye(128)
    for i in range(128):
        c[i, 1, i ^ 1] = 1.0
    c[:64, 2, :64] = 1.0
    c[64:, 2, 64:] = 1.0
    return c.astype(BF)


def col_vec(v, n):
    return np.ascontiguousarray(np.asarray(v, np.float32).reshape(n, 128).T)


_CACHE = {}


def get_prog(name, builder, *args):
    key = (name,) + tuple(args)
    if key not in _CACHE:
        _CACHE[key] = builder(*args)
    return _CACHE[key]


def run_P(layer, x, xctx, inp):
    nc = get_prog("P", build_P, layer)
    cosT, sinT = rope_tables()
    consts = make_consts()
    if layer == 0:
        glist = [inp['glob_q_norm'][0]] * 4 + [inp['glob_k_norm'][0]] + [np.ones(64, np.float32)] * 21
    else:
        glist = ([inp['win_q_norm'][0]] * 4 + [inp['win_k_norm'][0]] + [np.ones(64, np.float32)] + [inp['nat_q_norm'][0]] * 4
                 + [inp['nat_k_norm'][0]] * 4 + [np.ones(64, np.float32)] * 12)
    gains = np.ascontiguousarray(np.stack([np.tile(np.asarray(g, np.float32), 2) for g in glist], axis=1))
    b_ada = np.ascontiguousarray(np.repeat(col_vec(inp['b_ada'][layer], 24)[:, :, None], 2, axis=2))
    ng = col_vec(inp['norm_g'][layer], 8)
    cxs = np.ascontiguousarray(xctx.reshape(2 * CTX, D))
    w_ada = np.ascontiguousarray(inp['w_ada'][layer])
    w_in = np.ascontiguousarray(inp['w_in'][layer])
    maps = []
    for core in range(8):
        b, j = core // 4, core % 4
        cT = np.ascontiguousarray(np.stack([col_vec(inp['c'][b], 8), col_vec(inp['c_ctx'], 8)], axis=2))
        maps.append({
            "x": np.ascontiguousarray(x[b, j * TOK:(j + 1) * TOK]), "cx": cxs, "cT": cT, "w_ada": w_ada, "b_ada": b_ada, "ng": ng,
            "w_in": w_in, "gains": gains, "cosT": np.ascontiguousarray(cosT[:, j * TOK:(j + 1) * TOK]),
            "sinT": np.ascontiguousarray(sinT[:, j * TOK:(j + 1) * TOK]), "consts": consts,
        })
    import os
    if int(os.environ.get("PSTAGE", "9")) < 9:
        res = run_bass_kernel_spmd(nc, maps[:1], core_ids=[0]).results
        res = res * 8
    else:
        res = run_bass_kernel_spmd(nc, maps, core_ids=list(range(8))).results
    pT = np.empty((INW, 2, SEQ), BF)
    for core in range(8):
        b, j = core // 4, core % 4
        pT[:, b, j * TOK:(j + 1) * TOK] = res[core]["pT"][:, :TOK]
    pTc = np.ascontiguousarray(res[0]["pT"][:, TOK:].reshape(INW, 2, CTX))
    mod = np.stack([res[0]["modo"][:, :, 0], res[4]["modo"][:, :, 0]], axis=0)
    cmod = res[0]["modo"][:, :, 1]
    return pT, pTc, mod, cmod


def build_G(name, TQ, TW, NKB, NQS, NKS, NVS, npass, heads, blocks_fn, NE, use_sink):
    k = K()
    nc, P = k.nc, k.P
    NH = sum(len(h) for h in heads)
    qT = k.inp("qT", [npass, 128, NQS, TQ], BF16)
    kT = k.inp("kT", [npass, 128, NKS, NKB * 128], BF16)
    va = k.inp("va", [npass, 128, NKB, NVS, 128], BF16)
    shm = k.inp("shm", [128, 64], F32)
    if NE:
        Bt = k.inp("Bt", [128, NE, TW], F32)
    if use_sink:
        snk = k.inp("snk", [128, 8], F32)
    yT = k.outp("yT", [NH * 64, TQ], BF16)

    q_sb = k.sb("q_sb", [128, NQS, TQ], BF16)
    k_sb = k.sb("k_sb", [128, NKS, NKB * 128], BF16)
    v_sb = k.sb("v_sb", [128, NKB, NVS, 128], BF16)
    shs = k.sb("shs", [128, 64], F32)
    NPB = 4
    pT = [k.sb("pT%d" % i, [128, 512], BF16) for i in range(NPB)]
    R = [k.sb("R%d" % i, [128, TW], F32) for i in range(2)]
    Rs = [k.sb("Rs%d" % i, [64, TW], F32) for i in range(2)]
    yo = [k.sb("yo%d" % i, [64, TW], BF16) for i in range(2)]
    if NE:
        E_sb = k.sb("E_sb", [128, NE, TW], BF16)
        Bst = [k.sb("Bst%d" % i, [128, 8, TW], F32) for i in range(2)]
    if use_sink:
        es = k.sb("es", [128, 8], F32)
    NSB = 3
    S_ps = [k.ps("S%d" % i, [128, 512], F32) for i in range(NSB)]
    O_ps = [k.ps("O%d" % i, [128, 512], F32) for i in range(2)]
    Rp = [k.ps("Rp%d" % i, [128, 512], F32) for i in range(2)]

    P.dma("sp", shs[:], shm, writes=["shs"])
    for ri in range(2):
        P.op("dve", (lambda e, ri=ri: e.memset(R[ri][:], 0.0)), writes=["R%d" % ri])
    if use_sink:
        P.dma("sp", es[:], snk, writes=["es"])
        P.op("act", lambda e: e.activation(out=es[:], in_=es[:], func=AF.Exp), reads=["es"], writes=["es"])
    if NE:
        for i0 in range(0, NE, 8):
            n = min(8, NE - i0)
            bi = (i0 // 8) % 2
            P.dma("sp", Bst[bi][:, 0:n, :], Bt[:, i0:i0 + n, :], writes=["Bst%d" % bi])
            P.op("act", (lambda e, bi=bi, i0=i0, n=n: e.activation(out=E_sb[:, i0:i0 + n, :], in_=Bst[bi][:, 0:n, :], func=AF.Exp)),
                 reads=["Bst%d" % bi], writes=["E_sb"])

    ntile = TQ // TW
    GB = 512 // TW
    LA = 2
    groups = []
    units = []
    for ps_ in range(npass):
        for hd in heads[ps_]:
            for t in range(ntile):
                blks = blocks_fn(t)
                u = len(units)
                units.append((ps_, hd, t))
                ng = (len(blks) + GB - 1) // GB
                for gg in range(ng):
                    grp = blks[gg * GB:(gg + 1) * GB]
                    groups.append(dict(u=u, ps=ps_, hd=hd, t=t, grp=grp, first=(gg == 0), last=(gg == ng - 1),
                                       newpass=(gg == 0 and t == 0 and hd is heads[ps_][0])))

    def load_pass(ps_):
        for s_ in range(NQS):
            P.dma("sp", q_sb[:, s_, :], qT[ps_, :, s_, :], writes=["q_sb"])
        for s_ in range(NKS):
            P.dma("pool", k_sb[:, s_, :], kT[ps_, :, s_, :], writes=["k_sb"])
        step = max(1, 16 // NVS)
        for b0 in range(0, NKB, step):
            b1 = min(NKB, b0 + step)
            P.dma("sp" if (b0 // step) % 2 else "pool", v_sb[:, b0:b1], va[ps_, :, b0:b1], writes=["v_sb"])

    def emit_qk(n, G):
        (qs, base, ks, vs, oh, si, bh) = G["hd"]
        St = S_ps[n % NSB]
        stag = "PS:S%d" % (n % NSB)
        t = G["t"]
        for gi, (kb, ev) in enumerate(G["grp"]):
            P.op("pe", (lambda e, St=St, gi=gi, kb=kb, ks=ks, qs=qs, t=t: e.matmul(
                St[:, gi * TW:(gi + 1) * TW], lhsT=k_sb[:, ks, kb * 128:(kb + 1) * 128],
                rhs=q_sb[:, qs, t * TW:(t + 1) * TW], start=True, stop=True)),
                reads=["k_sb", "q_sb"], writes=[stag])

    def emit_rest(n, G):
        (qs, base, ks, vs, oh, si, bh) = G["hd"]
        St = S_ps[n % NSB]
        stag = "PS:S%d" % (n % NSB)
        pb_i = n % NPB
        ptag = "pT%d" % pb_i
        ob = G["u"] % 2
        Ot = O_ps[ob]
        otag = "PS:O%d" % ob
        grp = G["grp"]
        w = len(grp) * TW
        P.op("act", (lambda e, St=St, pb_i=pb_i, w=w: e.activation(out=pT[pb_i][:, 0:w], in_=St[:, 0:w], func=AF.Exp, scale=0.125)),
             reads=[stag], writes=[ptag])
        for gi, (kb, ev) in enumerate(grp):
            if ev is not None:
                ei = ev(bh)
                P.op("dve", (lambda e, pb_i=pb_i, gi=gi, ei=ei: e.tensor_tensor(
                    out=pT[pb_i][:, gi * TW:(gi + 1) * TW], in0=pT[pb_i][:, gi * TW:(gi + 1) * TW], in1=E_sb[:, ei, :], op=ALU.mult)),
                    reads=[ptag, "E_sb"], writes=[ptag])
        for gi, (kb, ev) in enumerate(grp):
            first = G["first"] and gi == 0
            last = G["last"] and gi == len(grp) - 1
            P.op("pe", (lambda e, Ot=Ot, kb=kb, vs=vs, pb_i=pb_i, gi=gi, first=first, last=last: e.matmul(
                Ot[:, 0:TW], lhsT=v_sb[:, kb, vs, :], rhs=pT[pb_i][:, gi * TW:(gi + 1) * TW], start=first, stop=last)),
                reads=["v_sb", ptag], writes=[otag])

    def emit_norm_a(u):
        (ps_, hd, t) = units[u]
        (qs, base, ks, vs, oh, si, bh) = hd
        Ot = O_ps[u % 2]
        otag = "PS:O%d" % (u % 2)
        ri = u % 2
        rt = "R%d" % ri
        if si is not None:
            P.op("dve", (lambda e, Ot=Ot, si=si, ri=ri: e.tensor_scalar(out=R[ri][64:128, :], in0=Ot[64:128, 0:TW], scalar1=es[64:128, si:si + 1],
                                                                   scalar2=None, op0=ALU.add)),
                 reads=[otag, "es"], writes=[rt])
            P.op("dve", (lambda e, ri=ri: e.reciprocal(out=R[ri][64:128, :], in_=R[ri][64:128, :])), reads=[rt], writes=[rt])
        else:
            P.op("dve", (lambda e, Ot=Ot, ri=ri: e.reciprocal(out=R[ri][64:128, :], in_=Ot[64:128, 0:TW])), reads=[otag], writes=[rt])

    def emit_norm_b(u):
        (ps_, hd, t) = units[u]
        (qs, base, ks, vs, oh, si, bh) = hd
        Ot = O_ps[u % 2]
        otag = "PS:O%d" % (u % 2)
        ri = u % 2
        P.op("pe", (lambda e, ri=ri: e.matmul(Rp[ri][0:64, 0:TW], lhsT=shs[:], rhs=R[ri][:], start=True, stop=True)),
             reads=["shs", "R%d" % ri], writes=["PS:Rp%d" % ri])
        P.op("act", (lambda e, ri=ri: e.activation(out=Rs[ri][:], in_=Rp[ri][0:64, 0:TW], func=AF.Identity)), reads=["PS:Rp%d" % ri], writes=["Rs%d" % ri])
        P.op("dve", (lambda e, ri=ri, Ot=Ot: e.tensor_tensor(out=yo[ri][:], in0=Ot[0:64, 0:TW], in1=Rs[ri][:], op=ALU.mult)),
             reads=[otag, "Rs%d" % ri], writes=["yo%d" % ri])
        P.dma("sp", yT[oh * 64:(oh + 1) * 64, t * TW:(t + 1) * TW], yo[ri][:], reads=["yo%d" % ri])

    gbase = 0
    for ps_ in range(npass):
        gl = [G for G in groups if G["ps"] == ps_]
        N = len(gl)
        load_pass(ps_)
        pend_b = {}
        for idx in range(N + LA + 4):
            if idx < N:
                emit_qk(gbase + idx, gl[idx])
            for u in pend_b.pop(idx, []):
                emit_norm_b(u)
            j = idx - LA
            if 0 <= j < N:
                G = gl[j]
                emit_rest(gbase + j, G)
                if G["last"]:
                    emit_norm_a(G["u"])
                    pend_b.setdefault(idx + 2, []).append(G["u"])
        gbase += N
    return k.finish()


def shift_mat():
    m = np.zeros((128, 64), np.float32)
    for i in range(64):
        m[64 + i, i] = 1.0
    return m


def make_vaug(vT_rows, nkb):
    nv = vT_rows.shape[0] // 64
    v = vT_rows.view(np.uint16).reshape(nv, 64, nkb, 128)
    out = np.empty((128, nkb, nv, 128), np.uint16)
    out[:, :, :, :64] = v.transpose(3, 2, 0, 1)
    out[:, :, :, 64:] = np.array([1.0], BF).view(np.uint16)[0]
    return out.view(BF)


def gqa_q_layout(qrows):
    T = qrows.shape[1]
    q = qrows.view(np.uint16).reshape(8, 64, T)
    out = np.zeros((128, 8, T), np.uint16)
    for h in range(8):
        g = h // 4
        out[64 * g:64 * g + 64, h] = q[h]
    return out.view(BF)


GQA_HEADS = [[(4 * g + i, 64 * g, 0, g, 4 * g + i, None, 0) for g in range(2) for i in range(4)]]


def run_G_global(pT, pTc):
    NKB = 2 + SEQ // 128
    blocks = [(kb, None) for kb in range(NKB)]
    nc = get_prog("Gglob", lambda: build_G("glob", TOK, 512, NKB, 8, 1, 2, 1, GQA_HEADS, lambda t: blocks, 0, False))
    maps = []
    sh = shift_mat()
    for b in range(2):
        kfull = np.concatenate([pTc[512:640, b], pT[512:640, b]], axis=1)
        vfull = np.concatenate([pTc[640:768, b], pT[640:768, b]], axis=1)
        kT = np.ascontiguousarray(kfull.reshape(1, 128, 1, NKB * 128))
        va = make_vaug(np.ascontiguousarray(vfull), NKB)[None]
        for j in range(4):
            q = gqa_q_layout(np.ascontiguousarray(pT[0:512, b, j * TOK:(j + 1) * TOK]))[None]
            maps.append({"qT": q, "kT": kT, "va": va, "shm": sh})
    res = run_bass_kernel_spmd(nc, maps, core_ids=list(range(8))).results
    yT = np.empty((512, 2, SEQ), BF)
    for core in range(8):
        yT[:, core // 4, (core % 4) * TOK:(core % 4 + 1) * TOK] = res[core]["yT"]
    return yT


def build_O(with_ctx):
    k = K()
    nc, P = k.nc, k.P
    NT = TOK + (2 * CTX if with_ctx else 0)
    yT = k.inp("yT", [D, NT], BF16)
    sgT = k.inp("sgT", [D, NT], BF16)
    x = k.inp("x", [NT, D], F32)
    grow = k.inp("grow", [128, 2, D], F32)
    w_out = k.inp("w_out", [D, D], F32)
    xo = k.outp("xo", [NT, D], F32)

    w_sb = k.sb("w_sb", [128, 8, D], BF16)
    wst = [k.sb("wst%d" % i, [128, D], F32) for i in range(2)]
    gr = k.sb("gr", [128, 2, D], F32)
    yb = [k.sb("yb%d" % i, [128, 8, 512], BF16) for i in range(2)]
    gb = [k.sb("gb%d" % i, [128, 8, 512], BF16) for i in range(2)]
    xt = [k.sb("xt%d" % i, [128, D], F32) for i in range(3)]
    tm = [k.sb("tm%d" % i, [128, D], F32) for i in range(2)]
    op_ = [k.ps("op%d" % i, [128, 512], F32) for i in range(4)]

    P.dma("sp", gr[:], grow, writes=["gr"])
    w_v = w_out.rearrange("(kc p) n -> p kc n", p=128)
    for kc in range(8):
        i = kc % 2
        P.dma("sp", wst[i][:], w_v[:, kc, :], writes=["wst%d" % i])
        P.op("dve" if i else "pool", (lambda e, i=i, kc=kc: e.tensor_copy(out=w_sb[:, kc, :], in_=wst[i][:])),
             reads=["wst%d" % i], writes=["w_sb%d" % kc])
    yv = yT.rearrange("(kc p) n -> p kc n", p=128)
    sv = sgT.rearrange("(kc p) n -> p kc n", p=128)
    nb = 0
    for t in range(NT // 512):
        bi = t % 2
        gsel = 1 if t >= TOK // 512 else 0
        P.dma("sp", yb[bi][:], yv[:, :, t * 512:(t + 1) * 512], writes=["yb%d" % bi] + ["yg%d_%d" % (bi, kc) for kc in range(8)])
        P.dma("pool", gb[bi][:], sv[:, :, t * 512:(t + 1) * 512], writes=["gb%d" % bi])
        for kc in range(8):
            P.op("pool" if kc % 2 else "dve", (lambda e, bi=bi, kc=kc: e.tensor_tensor(out=yb[bi][:, kc, :], in0=yb[bi][:, kc, :], in1=gb[bi][:, kc, :],
                                                                                      op=ALU.mult)),
                 reads=["yb%d" % bi, "gb%d" % bi], writes=["yg%d_%d" % (bi, kc)])
        for s in range(4):
            xi = nb % 3
            ti = nb % 2
            tok0 = t * 512 + s * 128
            P.dma("act", xt[xi][:], x[tok0:tok0 + 128, :], writes=["xt%d" % xi])
            for hh in range(2):
                pi = (nb * 2 + hh) % 4
                for kc in range(8):
                    P.op("pe", (lambda e, pi=pi, bi=bi, kc=kc, s=s, hh=hh: e.matmul(op_[pi][:], lhsT=yb[bi][:, kc, s * 128:(s + 1) * 128],
                                                                                    rhs=w_sb[:, kc, hh * 512:(hh + 1) * 512], start=(kc == 0), stop=(kc == 7))),
                         reads=["yg%d_%d" % (bi, kc), "w_sb%d" % kc], writes=["PS:op%d" % pi])
                P.op("dve", (lambda e, pi=pi, ti=ti, hh=hh, gsel=gsel: e.tensor_tensor(out=tm[ti][:, hh * 512:(hh + 1) * 512], in0=op_[pi][:],
                                                                                        in1=gr[:, gsel, hh * 512:(hh + 1) * 512], op=ALU.mult)),
                     reads=["PS:op%d" % pi, "gr"], writes=["tm%d_%d" % (ti, hh)])
            P.op("pool", (lambda e, ti=ti, xi=xi: e.tensor_tensor(out=tm[ti][:], in0=tm[ti][:], in1=xt[xi][:], op=ALU.add)),
                 reads=["tm%d_0" % ti, "tm%d_1" % ti, "xt%d" % xi], writes=["tm%d_0" % ti, "tm%d_1" % ti])
            P.dma("sp", xo[tok0:tok0 + 128, :], tm[ti][:], reads=["tm%d_0" % ti, "tm%d_1" % ti])
            nb += 1
    return k.finish()


def run_O(layer, x, xctx, yT, yTc, sgT, sgTc, mod, cmod, inp):
    with_ctx = yTc is not None
    nc = get_prog("O", build_O, with_ctx)
    w_out = np.ascontiguousarray(inp['w_out'][layer])
    maps = []
    for core in range(8):
        b, j = core // 4, core % 4
        gate = mod[b][:, 16:24].T.reshape(-1)
        cgate = cmod[:, 16:24].T.reshape(-1)
        grow = np.ascontiguousarray(np.broadcast_to(np.stack([gate, cgate])[None], (128, 2, D))).astype(np.float32)
        sl = slice(j * TOK, (j + 1) * TOK)
        if with_ctx:
            y_ = np.concatenate([yT[:, b, sl], yTc.reshape(D, 2 * CTX)], axis=1)
            s_ = np.concatenate([sgT[:, b, sl], sgTc.reshape(D, 2 * CTX)], axis=1)
            x_ = np.concatenate([x[b, sl], xctx.reshape(2 * CTX, D)], axis=0)
        else:
            y_, s_, x_ = yT[:, b, sl], sgT[:, b, sl], x[b, sl]
        maps.append({"yT": np.ascontiguousarray(y_), "sgT": np.ascontiguousarray(s_), "x": np.ascontiguousarray(x_), "grow": grow, "w_out": w_out})
    res = run_bass_kernel_spmd(nc, maps, core_ids=list(range(8))).results
    xn = np.empty((2, SEQ, D), np.float32)
    for core in range(8):
        xn[core // 4, (core % 4) * TOK:(core % 4 + 1) * TOK] = res[core]["xo"][:TOK]
    xcn = res[0]["xo"][TOK:].reshape(2, CTX, D).copy() if with_ctx else None
    return xn, xcn


MAGIC = 12582912.0
TWO_PI = 6.283185307179586
PI_LO = 3.1415925


def build_H(n):
    k = K()
    nc, P = k.nc, k.P
    A = n // 128
    CH = min(512, n)
    NCHK = n // CH
    PIECE = min(1024, n)
    NPIECE = n // PIECE
    uT = k.inp("uT", [2, 96, 2, n], BF16)
    cwb = k.inp("cwb", [2, 96, 4], F32)
    ZT = k.inp("ZT", [2, 33, n], F32)
    tpos = k.inp("tpos", [2, 1, n], F32)
    w1 = k.inp("w1", [33, 64], F32)
    w2 = k.inp("w2", [64, 64], F32)
    bfq = k.inp("bfq", [64, 3], F32)
    w3c = k.inp("w3c", [64, 2, 128], F32)
    ndl = k.inp("ndl", [1, 128], F32)
    skr = k.inp("skr", [128, 128], F32)
    cf = k.inp("cf", [128, 2, 128], F32)
    cb16 = k.inp("cb16", [128, 2, 128], BF16)
    ybt = k.outp("ybt", [2, 128, 32, 2, A], BF16)
    Hs_t = nc.dram_tensor("Hs", [128, 2 * n], BF16, kind="Internal")
    Hs = Hs_t.ap()

    cfs = k.sb("cfs", [128, 2, 128], F32)
    idb2 = k.sb("idb", [128, 2, 128], BF16)
    idb = idb2[:, 0, :]
    jrev = idb2[:, 1, :]
    zr = [k.sb("zr%d" % i, [128, 2, A], BF16) for i in range(2)]
    zts = [k.sb("zts%d" % i, [33, CH], F32) for i in range(2)]
    tps = [k.sb("tps%d" % i, [1, CH], F32) for i in range(2)]
    w1s = k.sb("w1s", [33, 64], F32)
    w2s = k.sb("w2s", [64, 64], F32)
    bfs = k.sb("bfs", [64, 3], F32)
    w3s = k.sb("w3s", [64, 2, 128], F32)
    nds = k.sb("nds", [1, 128], F32)
    sks = k.sb("sks", [128, 128], F32)
    a1 = [k.sb("a1_%d" % i, [64, CH], F32) for i in range(2)]
    k1 = [k.sb("k1_%d" % i, [64, CH], F32) for i in range(2)]
    hd1 = [k.sb("hd1_%d" % i, [64, CH], F32) for i in range(2)]
    hd2 = [k.sb("hd2_%d" % i, [64, CH], F32) for i in range(2)]
    dec = [k.sb("dec%d" % i, [128, CH], F32) for i in range(2)]
    tb = [k.sb("tb%d" % i, [128, CH], BF16) for i in range(2)]
    jk = k.sb("jk", [128, CH], BF16)
    ssp = k.sb("ssp", [128, 2 * NCHK], F32)
    ssq = k.sb("ssq", [128, 1], F32)
    dg = k.sb("dg", [128, 128], F32)
    Srep = k.sb("Srep", [128, 128], F32)
    cws = k.sb("cws", [96, 4], F32)
    ut = [k.sb("ut%d" % i, [96, PIECE + 2], BF16) for i in range(2)]
    ac = [k.sb("ac%d" % i, [96, PIECE], F32) for i in range(2)]
    ucv = [k.sb("ucv%d" % i, [96, PIECE], BF16) for i in range(2)]
    Zt = k.sb("Zt", [128, 3, 32, 2, A], BF16)
    Gf = [k.sb("Gf%d" % i, [128, n], BF16) for i in range(2)]
    Gb = [k.sb("Gb%d" % i, [128, n], BF16) for i in range(1)]
    e1 = [k.sb("e1_%d" % i, [128, 2, A], F32) for i in range(2)]
    e2 = [k.sb("e2_%d" % i, [128, 2, A], F32) for i in range(2)]
    pA = k.ps("pA", [128, 512], F32)
    pB = k.ps("pB", [128, 512], F32)
    pC = k.ps("pC", [128, 512], F32)
    pD = k.ps("pD", [128, 512], F32)
    tpz = [k.ps("tpz%d" % i, [128, 1024], BF16) for i in range(2)]
    Yp = [k.ps("Yp%d" % i, [128, 512], F32) for i in range(2)]

    P.dma("sp", cfs[:], cf, writes=["cfs"])
    P.dma("sp", idb2[:], cb16, writes=["idb"])
    P.dma("sp", w1s[:], w1, writes=["w1s"])
    P.dma("sp", w2s[:], w2, writes=["w2s"])
    P.dma("sp", bfs[:], bfq, writes=["bfs"])
    P.dma("sp", w3s[:], w3c, writes=["w3s"])
    P.dma("sp", nds[:], ndl, writes=["nds"])
    P.dma("sp", sks[:], skr, writes=["sks"])

    def sin_layer(ps_t, ptag, bcol, out_t, otag, i):
        P.op("dve", (lambda e: e.tensor_scalar(out=a1[i][:], in0=ps_t, scalar1=bfs[:, bcol:bcol + 1], scalar2=bfs[:, 2:3], op0=ALU.add, op1=ALU.mult)),
             reads=[ptag, "bfs"], writes=["a1_%d" % i])
        P.op("pool", (lambda e: e.tensor_scalar(out=k1[i][:], in0=a1[i][:], scalar1=1.0 / TWO_PI, scalar2=MAGIC, op0=ALU.mult, op1=ALU.add)),
             reads=["a1_%d" % i], writes=["k1_%d" % i])
        P.op("pool", (lambda e: e.tensor_scalar(out=k1[i][:], in0=k1[i][:], scalar1=-MAGIC, scalar2=-TWO_PI, op0=ALU.add, op1=ALU.mult)),
             reads=["k1_%d" % i], writes=["k1_%d" % i])
        P.op("dve", (lambda e: e.tensor_tensor(out=a1[i][:], in0=a1[i][:], in1=k1[i][:], op=ALU.add)),
             reads=["a1_%d" % i, "k1_%d" % i], writes=["a1_%d" % i])
        P.op("pool", (lambda e: e.tensor_scalar(out=a1[i][:], in0=a1[i][:], scalar1=-PI_LO, scalar2=PI_LO, op0=ALU.max, op1=ALU.min)),
             reads=["a1_%d" % i], writes=["a1_%d" % i])
        P.op("act", (lambda e: e.activation(out=out_t[:], in_=a1[i][:], func=AF.Sin)), reads=["a1_%d" % i], writes=[otag])

    for dr in range(2):
        for c in range(NCHK):
            i = c % 2
            sl = slice(c * CH, (c + 1) * CH)
            P.dma("act", zts[i][:], ZT[dr, :, sl], writes=["zts%d" % i])
            P.dma("act", tps[i][:], tpos[dr, :, sl], writes=["tps%d" % i])
            P.op("pe", (lambda e, i=i: e.matmul(pA[0:64, 0:CH], lhsT=w1s[:], rhs=zts[i][:], start=True, stop=True)),
                 reads=["w1s", "zts%d" % i], writes=["PS:pA"])
            sin_layer(pA[0:64, 0:CH], "PS:pA", 0, hd1[i], "hd1_%d" % i, i)
            P.op("pe", (lambda e, i=i: e.matmul(pB[0:64, 0:CH], lhsT=w2s[:], rhs=hd1[i][:], start=True, stop=True)),
                 reads=["w2s", "hd1_%d" % i], writes=["PS:pB"])
            sin_layer(pB[0:64, 0:CH], "PS:pB", 1, hd2[i], "hd2_%d" % i, i)
            P.op("pe", (lambda e, i=i, dr=dr: e.matmul(pC[:, 0:CH], lhsT=w3s[:, dr, :], rhs=hd2[i][:], start=True, stop=True)),
                 reads=["w3s", "hd2_%d" % i], writes=["PS:pC"])
            P.op("pe", (lambda e, i=i: e.matmul(pD[:, 0:CH], lhsT=nds[:], rhs=tps[i][:], start=True, stop=True)),
                 reads=["nds", "tps%d" % i], writes=["PS:pD"])
            P.op("act", (lambda e, i=i: e.activation(out=dec[i][:], in_=pD[:, 0:CH], func=AF.Exp)), reads=["PS:pD"], writes=["dec%d" % i])
            P.op("dve", (lambda e, i=i: e.tensor_tensor(out=tb[i][:], in0=pC[:, 0:CH], in1=dec[i][:], op=ALU.mult)),
                 reads=["PS:pC", "dec%d" % i], writes=["tb%d" % i])
            lo = 1 if (dr == 1 and c == 0) else 0
            col = dr * NCHK + c
            P.op("act", (lambda e, i=i, lo=lo, col=col: e.activation(out=jk[:, lo:CH], in_=tb[i][:, lo:CH], func=AF.Square, accum_out=ssp[:, col:col + 1])),
                 reads=["tb%d" % i], writes=["jk", "ssp"])
            off = n if dr == 0 else 0
            P.dma("sp", Hs[:, off + c * CH: off + (c + 1) * CH], tb[i][:], reads=["tb%d" % i], writes=["Hs"])
    P.op("dve", lambda e: e.tensor_reduce(out=ssq[:], in_=ssp[:], axis=AX.X, op=ALU.add), reads=["ssp"], writes=["ssq"])
    P.op("act", lambda e: e.activation(out=ssq[:], in_=ssq[:], func=AF.Sqrt, bias=EPS), reads=["ssq"], writes=["ssq"])
    P.op("dve", lambda e: e.reciprocal(out=ssq[:], in_=ssq[:]), reads=["ssq"], writes=["ssq"])
    P.op("dve", lambda e: e.tensor_scalar(out=dg[:], in0=cfs[:, 0, :], scalar1=ssq[:], scalar2=None, op0=ALU.mult), reads=["cfs", "ssq"], writes=["dg"])
    P.op("pe", lambda e: e.matmul(pC[:, 0:128], lhsT=cfs[:, 1, :], rhs=dg[:], start=True, stop=True), reads=["cfs", "dg"], writes=["PS:pC"])
    P.op("dve", lambda e: e.tensor_copy(out=Srep[:], in_=pC[:, 0:128]), reads=["PS:pC"], writes=["Srep"])

    npc = 0
    ntr = 0
    ny = 0
    for g in range(2):
        P.dma("sp", cws[:], cwb[g], writes=["cws"])
        for b in range(2):
            for pc in range(NPIECE):
                ui = npc % 2
                t0 = pc * PIECE
                P.op("pool", (lambda e, ui=ui: e.memset(ut[ui][:, 0:1], 0.0)), writes=["ut%d" % ui])
                P.op("pool", (lambda e, ui=ui: e.memset(ut[ui][:, PIECE + 1:PIECE + 2], 0.0)), writes=["ut%d" % ui])
                lo = t0 - 1 if pc > 0 else t0
                hi = t0 + PIECE + 1 if pc < NPIECE - 1 else t0 + PIECE
                P.dma("sp", ut[ui][:, 1 + (lo - t0): 1 + (hi - t0)], uT[g, :, b, lo:hi], writes=["ut%d" % ui])
                P.op("dve", (lambda e, ui=ui: e.tensor_scalar(out=ac[ui][:], in0=ut[ui][:, 1:PIECE + 1], scalar1=cws[:, 1:2], scalar2=cws[:, 3:4],
                                                            op0=ALU.mult, op1=ALU.add)),
                     reads=["ut%d" % ui, "cws"], writes=["ac%d" % ui])
                P.op("dve", (lambda e, ui=ui: e.scalar_tensor_tensor(out=ac[ui][:], in0=ut[ui][:, 0:PIECE], scalar=cws[:, 0:1], in1=ac[ui][:],
                                                                   op0=ALU.mult, op1=ALU.add)),
                     reads=["ut%d" % ui, "cws", "ac%d" % ui], writes=["ac%d" % ui])
                P.op("dve", (lambda e, ui=ui: e.scalar_tensor_tensor(out=ucv[ui][:], in0=ut[ui][:, 2:PIECE + 2], scalar=cws[:, 2:3], in1=ac[ui][:],
                                                                   op0=ALU.mult, op1=ALU.add)),
                     reads=["ut%d" % ui, "cws", "ac%d" % ui], writes=["ucv%d" % ui])
                for s0 in range(0, PIECE // 128, 8):
                    ti = ntr % 2
                    ns = min(8, PIECE // 128 - s0)
                    for s in range(ns):
                        P.op("pe", (lambda e, ti=ti, s=s, s0=s0, ui=ui: e.transpose(tpz[ti][:, s * 128: s * 128 + 96],
                                                                                  ucv[ui][:, (s0 + s) * 128:(s0 + s + 1) * 128], idb[0:96, 0:96])),
                             reads=["ucv%d" % ui, "idb"], writes=["PS:tpz%d" % ti])
                    a0 = (t0 // 128) + s0
                    src = tpz[ti][:, 0:ns * 128].rearrange("p (s c) -> p c s", c=128)[:, 0:96, :]
                    dst = Zt[:, :, :, b, a0:a0 + ns].rearrange("p s c a -> p (s c) a")
                    P.op("act" if ti else "dve", (lambda e, src=src, dst=dst, ti=ti: (e.activation(out=dst, in_=src, func=AF.Identity) if ti
                                                                                       else e.tensor_copy(out=dst, in_=src))),
                         reads=["PS:tpz%d" % ti], writes=["Zt0", "Zt1", "Zt2"])
                    ntr += 1
                npc += 1
        for o in range(2):
            for c in range(32):
                row = o * 64 + g * 32 + c
                gi = ny % 2
                yp = Yp[gi]
                ytag = "PS:Yp%d" % gi
                base = Hs_t.ap()[row:row + 1, :]
                srcF = bass.AP(tensor=base.tensor, offset=base.offset + n - 127, ap=[[1, 128], [1, n]])
                srcB = bass.AP(tensor=base.tensor, offset=base.offset + 1, ap=[[1, 128], [1, n - 128]])
                P.dma("sp", Gf[gi][:], srcF, reads=["Hs"], writes=["Gf%d" % gi])
                if A > 1:
                    P.dma("act", Gb[0][:, 0:n - 128], srcB, reads=["Hs"], writes=["Gb0"])
                zin = Zt[:, o, c]
                yv = yp[:, 0:2 * A].rearrange("p (b a) -> p b a", b=2)
                pDv = pD[:, 0:2 * A].rearrange("p (b a) -> p b a", b=2)
                P.op("pe", (lambda e, zin=zin, pDv=pDv: e.matmul(pDv, lhsT=jrev, rhs=zin, start=True, stop=True)),
                     reads=["idb", "Zt%d" % o], writes=["PS:pD"])
                P.op("act", (lambda e, gi=gi, pDv=pDv: e.activation(out=zr[gi][:], in_=pDv, func=AF.Identity)),
                     reads=["PS:pD"], writes=["zr%d" % gi])
                zrv = zr[gi]
                nmm = 2 * A - 1
                im = 0
                for d in range(A):
                    P.op("pe", (lambda e, d=d, gi=gi, yv=yv, zrv=zrv, im=im: e.matmul(yv[:, :, d:A], lhsT=Gf[gi][:, d * 128:(d + 1) * 128],
                                                                                     rhs=zrv[:, :, 0:A - d], start=(im == 0), stop=(im == nmm - 1))),
                         reads=["Gf%d" % gi, "zr%d" % gi], writes=[ytag])
                    im += 1
                for ee in range(1, A):
                    P.op("pe", (lambda e, ee=ee, yv=yv, zrv=zrv, im=im: e.matmul(yv[:, :, 0:A - ee], lhsT=Gb[0][:, (A - 1 - ee) * 128:(A - ee) * 128],
                                                                                 rhs=zrv[:, :, ee:A], start=False, stop=(im == nmm - 1))),
                         reads=["Gb0", "zr%d" % gi], writes=[ytag])
                    im += 1
                col = o * 64 + g * 32 + c
                P.op("act", (lambda e, gi=gi, yv=yv, col=col: e.activation(out=e1[gi][:], in_=yv, func=AF.Identity, scale=Srep[:, col:col + 1])),
                     reads=[ytag, "Srep"], writes=["e1_%d" % gi])
                P.op("dve", (lambda e, gi=gi, zin=zin, col=col: e.scalar_tensor_tensor(out=e2[gi][:], in0=zin, scalar=sks[:, col:col + 1], in1=e1[gi][:],
                                                                                       op0=ALU.mult, op1=ALU.add)),
                     reads=["Zt%d" % o, "sks", "e1_%d" % gi], writes=["e2_%d" % gi])
                P.op("pool", (lambda e, gi=gi, o=o, c=c: e.tensor_tensor(out=Zt[:, o + 1, c], in0=Zt[:, o + 1, c], in1=e2[gi][:], op=ALU.mult)),
                     reads=["Zt%d" % (o + 1), "e2_%d" % gi], writes=["Zt%d" % (o + 1)])
                ny += 1
        P.dma("sp", ybt[g], Zt[:, 2], reads=["Zt2"])
    return k.finish()


def hyena_consts(n):
    m_f = np.arange(n)
    m_r = np.concatenate([[0], n - np.arange(1, n)])
    t_lin = np.linspace(0.0, 1.0, n, dtype=np.float32)
    f = np.linspace(1e-4, 15.0, 16, dtype=np.float32)[None, :]
    ZT = np.empty((2, 33, n), np.float32)
    tpos = np.empty((2, 1, n), np.float32)
    for i, m in enumerate((m_f, m_r)):
        w = (np.float32(2.0 * np.pi) * m.astype(np.float32)[:, None] / np.float32(n)).astype(np.float32)
        fw = (f * w).astype(np.float32)
        z = np.concatenate([t_lin[m][:, None], np.cos(fw), -np.sin(fw)], axis=-1).astype(np.float32)
        ZT[i] = z.T
        tpos[i, 0] = t_lin[m]
    deltas = np.linspace(np.log(1e-2) / 1.5, np.log(1e-2) / 0.3, 512, dtype=np.float32)
    return ZT, tpos, np.abs(deltas).astype(np.float32)


def run_H(n, uT_all, inp):
    nc = get_prog("H", build_H, n)
    A = n // 128
    ZT, tpos, adel = hyena_consts(n)
    cw = inp['hy_conv_w'][0]
    cbias = inp['hy_conv_b'][0]
    w3 = inp['hy_w3'][0]
    skip = inp['hy_skip'][0]
    cf = np.zeros((128, 2, 128), np.float32)
    cf[:, 0, :] = np.eye(128)
    cf[:, 1, :] = 1.0
    cb16 = np.ascontiguousarray(np.stack([np.eye(128, dtype=np.float32), np.eye(128, dtype=np.float32)[::-1]], axis=1)).astype(BF)
    bfq = np.ascontiguousarray(np.stack([inp['hy_b1'][0], inp['hy_b2'][0], inp['hy_freq'][0]], axis=1).astype(np.float32))
    u16 = uT_all.view(np.uint16)
    maps = []
    for core in range(8):
        ch0 = 64 * core
        uT = np.empty((2, 96, 2, n), np.uint16)
        cwb = np.empty((2, 96, 4), np.float32)
        for g in range(2):
            for s in range(3):
                r0 = s * 512 + ch0 + g * 32
                uT[g, s * 32:(s + 1) * 32] = u16[r0:r0 + 32]
                cwb[g, s * 32:(s + 1) * 32, 0:3] = cw[:, r0:r0 + 32].T
                cwb[g, s * 32:(s + 1) * 32, 3] = cbias[r0:r0 + 32]
        w3c = np.empty((64, 2, 128), np.float32)
        for dr in range(2):
            for o in range(2):
                c0 = o * 1024 + dr * 512 + ch0
                w3c[:, dr, o * 64:(o + 1) * 64] = w3[:, c0:c0 + 64]
        ndl = -np.tile(adel[ch0:ch0 + 64], 2)[None, :].astype(np.float32)
        skr = np.ascontiguousarray(np.broadcast_to(np.concatenate([skip[0, ch0:ch0 + 64], skip[1, ch0:ch0 + 64]])[None, :], (128, 128))).astype(np.float32)
        maps.append({"uT": uT.view(BF), "cwb": cwb, "ZT": ZT, "tpos": tpos, "w1": np.ascontiguousarray(inp['hy_w1'][0]),
                     "w2": np.ascontiguousarray(inp['hy_w2'][0]), "bfq": bfq, "w3c": w3c, "ndl": ndl, "skr": skr, "cf": cf, "cb16": cb16})
    res = run_bass_kernel_spmd(nc, maps, core_ids=list(range(8))).results
    ybT = np.empty((512, 2, n), np.uint16)
    for core in range(8):
        o = res[core]["ybt"].view(np.uint16)
        ybT[64 * core:64 * core + 64] = o.transpose(0, 2, 3, 4, 1).reshape(64, 2, n)
    return ybT.view(BF)


def run_G_ctx(pTc):
    blocks = [(0, None), (1, None)]
    nc = get_prog("Gctx", lambda: build_G("ctx", CTX, 128, 2, 8, 1, 2, 1, GQA_HEADS, lambda t: blocks, 0, False))
    maps = []
    sh = shift_mat()
    for b in range(2):
        kT = np.ascontiguousarray(pTc[512:640, b]).reshape(1, 128, 1, CTX)
        va = make_vaug(np.ascontiguousarray(pTc[640:768, b]), 2)[None]
        q = gqa_q_layout(np.ascontiguousarray(pTc[0:512, b]))[None]
        maps.append({"qT": q, "kT": kT, "va": va, "shm": sh})
    res = run_bass_kernel_spmd(nc, maps, core_ids=[0, 1]).results
    return np.ascontiguousarray(np.stack([res[0]["yT"], res[1]["yT"]], axis=1))


def halo_rows(rows_main, rows_ctx, j, H):
    R = rows_main.shape[0]
    out = np.zeros((R, CTX + (2 * H + 32) * 128), np.uint16)
    out[:, :CTX] = rows_ctx
    lo = j * TOK - 128 * H
    hi = (j + 1) * TOK + 128 * H
    slo, shi = max(lo, 0), min(hi, SEQ)
    out[:, CTX + (slo - lo): CTX + (shi - lo)] = rows_main[:, slo:shi]
    return out


NEGB = -30000.0


def win_tables(j):
    B = np.full((128, 9, 128), NEGB, np.float32)
    r = np.arange(128)[:, None]
    cq = np.arange(128)[None, :]
    for v, t in enumerate((0, 1, 31)):
        gi = 32 * j + t
        for o in range(3):
            gk = gi + o - 1
            if 0 <= gk < SEQ // 128:
                ok = np.abs(128 * (o - 1) + r - cq) <= 128
                B[:, v * 3 + o, :] = np.where(ok, 0.0, NEGB)
    return B


def run_G_win(pT, pTc, inp):
    NKB = 2 + 34

    def blocks(t):
        v = 0 if t == 0 else (2 if t == 31 else 1)
        return [(0, None), (1, None)] + [(2 + t + o, (lambda bh, v=v, o=o: v * 3 + o)) for o in range(3)]
    heads = [[(4 * g + i, 64 * g, 0, g, 4 * g + i, 4 * g + i, 0) for g in range(2) for i in range(4)]]
    nc = get_prog("Gwin", lambda: build_G("win", TOK, 128, NKB, 8, 1, 2, 1, heads, blocks, 9, True))
    sh = shift_mat()
    snk = np.ascontiguousarray(np.broadcast_to(np.asarray(inp['win_sink'][0], np.float32)[None, :], (128, 8)))
    p16, c16 = pT.view(np.uint16), pTc.view(np.uint16)
    maps = []
    for core in range(8):
        b, j = core // 4, core % 4
        kk = halo_rows(p16[512:640, b], c16[512:640, b], j, 1)
        vv = halo_rows(p16[640:768, b], c16[640:768, b], j, 1)
        q = gqa_q_layout(np.ascontiguousarray(pT[0:512, b, j * TOK:(j + 1) * TOK]))[None]
        maps.append({"qT": q, "kT": kk.view(BF).reshape(1, 128, 1, NKB * 128), "va": make_vaug(vv.view(BF), NKB)[None], "shm": sh,
                     "Bt": win_tables(j), "snk": snk})
    res = run_bass_kernel_spmd(nc, maps, core_ids=list(range(8))).results
    yT = np.empty((512, 2, SEQ), BF)
    for core in range(8):
        yT[:, core // 4, (core % 4) * TOK:(core % 4 + 1) * TOK] = res[core]["yT"]
    return yT


def na_tables(j, rpb):
    B = np.full((128, 216, 128), NEGB, np.float32)
    r = np.arange(128)[:, None]
    cq = np.arange(128)[None, :]
    for v, t in enumerate((0, 1, 2, 30, 31)):
        gi = 32 * j + t
        qr = 2 * gi + cq // 64
        qc = cq % 64
        r0 = np.clip(qr - 4, 0, 256 - 8)
        c0 = np.clip(qc - 8, 0, 64 - 16)
        olist = list(range(5)) + ([5] if v == 0 else []) + ([-1] if v == 4 else [])
        for o in olist:
            gk = gi + o - 2
            if not (0 <= gk < SEQ // 128):
                continue
            kr = 2 * gk + r // 64
            kc = r % 64
            ok = (kr >= r0) & (kr < r0 + 8) & (kc >= c0) & (kc < c0 + 16)
            ri = np.clip(kr - qr + 7, 0, 14)
            ci = np.clip(kc - qc + 15, 0, 30)
            for h in range(8):
                idx = (v * 8 + h) * 5 + o if 0 <= o < 5 else (200 + h if o == 5 else 208 + h)
                B[:, idx, :] = np.where(ok, rpb[h][ri, ci], NEGB)
    return B


def run_G_na(pT, pTc, inp):
    NKB = 2 + 36

    def blocks(t):
        v = {0: 0, 1: 1, 30: 3, 31: 4}.get(t, 2)
        bl = [(0, None), (1, None)] + [(2 + t + o, (lambda bh, v=v, o=o: (v * 8 + bh) * 5 + o)) for o in range(5)]
        if t == 0:
            bl.append((2 + t + 5, (lambda bh: 200 + bh)))
        if t == 31:
            bl.append((2 + t - 1, (lambda bh: 208 + bh)))
        return bl
    heads = [[(h % 4, 64 * (h % 2), (h // 2) % 2, h % 4, h, None, h) for h in range(4 * p, 4 * p + 4)] for p in range(2)]
    nc = get_prog("Gna", lambda: build_G("na", TOK, 128, NKB, 4, 2, 4, 2, heads, blocks, 216, False))
    sh = shift_mat()
    rpb = np.asarray(inp['nat_rpb'][0], np.float32)
    p16, c16 = pT.view(np.uint16), pTc.view(np.uint16)
    maps = []
    for core in range(8):
        b, j = core // 4, core % 4
        qs, ks, vs = [], [], []
        for p in range(2):
            q = p16[768 + 256 * p: 768 + 256 * (p + 1), b, j * TOK:(j + 1) * TOK].reshape(4, 64, TOK)
            qz = np.zeros((128, 4, TOK), np.uint16)
            for hh in range(4):
                qz[64 * (hh % 2):64 * (hh % 2) + 64, hh] = q[hh]
            qs.append(qz)
            kk = halo_rows(p16[1280 + 256 * p: 1280 + 256 * (p + 1), b], c16[1280 + 256 * p: 1280 + 256 * (p + 1), b], j, 2)
            ks.append(np.ascontiguousarray(kk.reshape(2, 128, NKB * 128).transpose(1, 0, 2)))
            vv = halo_rows(p16[1792 + 256 * p: 1792 + 256 * (p + 1), b], c16[1792 + 256 * p: 1792 + 256 * (p + 1), b], j, 2)
            vs.append(make_vaug(vv.view(BF), NKB))
        maps.append({"qT": np.stack(qs).view(BF), "kT": np.stack(ks).view(BF), "va": np.stack(vs), "shm": sh, "Bt": na_tables(j, rpb)})
    res = run_bass_kernel_spmd(nc, maps, core_ids=list(range(8))).results
    yT = np.empty((512, 2, SEQ), BF)
    for core in range(8):
        yT[:, core // 4, (core % 4) * TOK:(core % 4 + 1) * TOK] = res[core]["yT"]
    return yT


def kernel(**inp):
    inp = {k_: np.asarray(v) for k_, v in inp.items()}
    x = np.ascontiguousarray(inp['x'], dtype=np.float32)
    xctx = np.ascontiguousarray(inp['ctx'], dtype=np.float32)
    pT, pTc, mod, cmod = run_P(0, x, xctx, inp)
    ya = run_G_global(pT, pTc)
    yca = run_G_ctx(pTc)
    yb = run_H(SEQ, np.ascontiguousarray(pT[768:2304]), inp)
    ycb = run_H(CTX, np.ascontiguousarray(pTc[768:2304]), inp)
    yT = np.concatenate([ya, yb], axis=0)
    yTc = np.concatenate([yca, ycb], axis=0)
    x, xctx = run_O(0, x, xctx, yT, yTc, pT[2304:3328], pTc[2304:3328], mod, cmod, inp)
    pT, pTc, mod, cmod = run_P(1, x, xctx, inp)
    yw = run_G_win(pT, pTc, inp)
    yn = run_G_na(pT, pTc, inp)
    yT = np.concatenate([yw, yn], axis=0)
    x, _ = run_O(1, x, xctx, yT, None, pT[2304:3328], None, mod, cmod, inp)
    return x.astype(np.float32)
```
